# Optimizing a Trainium2 kernel written in Bass

```python
import jax, jax.numpy as jnp
from jax import lax
import numpy as np

D_MODEL = 2048
BATCH = 2
SEQ = 16384
DEPTH = 1
DEC_BATCH = 16
DEC_SEQ = 2048
PAST_LEN = 128

HEAD_DIM = 64
N_HEADS_A = 16
D_A = N_HEADS_A * HEAD_DIM
D_B = D_MODEL // 2
CONV_W = 3
LORA_W = max(32, int(round(1.8 * D_MODEL ** 0.5 / 32)) * 32)
LORA_A = max(32, int(round(1.8 * D_MODEL ** 0.5 / 32)) * 32)
N_DIR = 2
RMS_EPS = 1e-6
GN_EPS = 64e-5
SHIFT_SIZES = (D_A, D_A, D_A, LORA_W, LORA_W, LORA_A, LORA_A)
REST_SIZES = (D_A, D_B, D_B, D_B, D_B, D_MODEL, D_MODEL)
N_SHIFT = 3 * D_A + 2 * LORA_W + 2 * LORA_A
N_COLS = N_SHIFT + D_A + 4 * D_B + 2 * D_MODEL

kernel_name = "rwkv7_shortconv_gated_parallel_encoder"


def _offsets(sizes):
    out, acc = [], 0
    for s in sizes[:-1]:
        acc += s
        out.append(acc)
    return out


def _rmsnorm(x, w):
    x32 = x.astype(jnp.float32)
    y = x32 * lax.rsqrt(jnp.mean(x32 * x32, axis=-1, keepdims=True) + RMS_EPS)
    return (y * w.astype(jnp.float32)).astype(x.dtype)


def _centred_shift_mix(z, mu):
    zp = jnp.pad(z, ((0, 0), (1, 1), (0, 0)))
    return z + mu * (0.5 * (zp[:, :-2] + zp[:, 2:]) - z)


def _rwkv7_bidir(r, k, v, wd, ad, w0, w_up, a0, a_up, k_k, k_a, r_k, ln_w, ln_b):
    bsz, t_len, _ = r.shape
    hs = (bsz, t_len, N_HEADS_A, HEAD_DIM)
    w_log = -jax.nn.softplus(-(w0[:, None, None, :] + jnp.einsum('zbtl,zlc->zbtc', jnp.tanh(wd), w_up))) - 0.5
    decay = jnp.exp(-jnp.exp(w_log))
    a = jax.nn.sigmoid(a0[:, None, None, :] + jnp.einsum('zbtl,zlc->zbtc', ad, a_up))
    kk = (k * k_k).reshape(hs)
    kk = (kk / jnp.maximum(jnp.linalg.norm(kk, axis=-1, keepdims=True), 1e-12)).reshape(r.shape)
    k_mod = k * (1.0 + (a - 1.0) * k_a)
    kk_a = kk * a

    def orient(u):
        return jnp.stack([u[0], jnp.flip(u[1], axis=1)])

    def both(u):
        return jnp.stack([u, jnp.flip(u, axis=1)])

    def tm(u):
        return jnp.moveaxis(u.reshape((N_DIR,) + hs), 2, 0)

    xs = (tm(both(r)), tm(orient(decay)), tm(orient(k_mod)), tm(both(v)), tm(both(kk)), tm(orient(kk_a)))

    def step(state, inp):
        r_t, w_t, k_t, v_t, kk_t, b_t = inp
        sa = jnp.einsum('zbhvk,zbhk->zbhv', state, kk_t)
        state = (state * w_t[..., None, :] - sa[..., :, None] * b_t[..., None, :]
                 + v_t[..., :, None] * k_t[..., None, :])
        return state, jnp.einsum('zbhvk,zbhk->zbhv', state, r_t)

    s0 = jnp.zeros((N_DIR, bsz, N_HEADS_A, HEAD_DIM, HEAD_DIM), jnp.float32)
    _, ys = lax.scan(step, s0, xs)
    ys = jnp.moveaxis(ys, 0, 2)
    ys = jnp.stack([ys[0], jnp.flip(ys[1], axis=1)])
    mean = jnp.mean(ys, axis=-1, keepdims=True)
    var = jnp.mean(jnp.square(ys - mean), axis=-1, keepdims=True)
    ys = (ys - mean) * lax.rsqrt(var + GN_EPS) * ln_w.reshape(N_HEADS_A, HEAD_DIM) + ln_b.reshape(N_HEADS_A, HEAD_DIM)
    bonus = jnp.sum(r.reshape(hs) * k_mod.reshape((N_DIR,) + hs) * r_k, axis=-1, keepdims=True) * v.reshape(hs)
    return jnp.sum(ys + bonus, axis=0).reshape(r.shape)


def _short_conv(b_g, c_g, h_b, conv_w):
    u = c_g * h_b
    y = lax.conv_general_dilated(u, conv_w[:, None, :], window_strides=(1,), padding=((CONV_W // 2, CONV_W // 2),),
                                 dimension_numbers=('NWC', 'WIO', 'NWC'), feature_group_count=D_B)
    return b_g * y


def _layer(x, norm_w, w_in, gate_bias, mu_shift, w0, w_up, a0, a_up, k_k, k_a, r_k, ln_w, ln_b,
           w_a_out, conv_w, w_b_out, w_o):
    h = _rmsnorm(x, norm_w)
    z = jnp.einsum('btd,dn->btn', h, w_in)
    zs = _centred_shift_mix(z[..., :N_SHIFT], mu_shift).astype(jnp.float32)
    r, k, v, wd_f, wd_b, ad_f, ad_b = jnp.split(zs, _offsets(SHIFT_SIZES), axis=-1)
    g_a, b_g, c_g, h_b, g_b, m_a, m_b = jnp.split(z[..., N_SHIFT:], _offsets(REST_SIZES), axis=-1)
    f32 = jnp.float32
    y_a = _rwkv7_bidir(r, k, v, jnp.stack([wd_f, wd_b]), jnp.stack([ad_f, ad_b]),
                       w0.astype(f32), w_up.astype(f32), a0.astype(f32), a_up.astype(f32),
                       k_k.astype(f32), k_a.astype(f32), r_k.astype(f32), ln_w.astype(f32), ln_b.astype(f32))
    o_a = jnp.einsum('btc,cd->btd', y_a.astype(x.dtype) * jax.nn.silu(g_a), w_a_out)
    y_b = _short_conv(b_g, c_g, h_b, conv_w)
    o_b = jnp.einsum('btc,cd->btd', y_b * jax.nn.silu(g_b), w_b_out)
    merged = jax.nn.sigmoid(m_a + gate_bias[0]) * o_a + jax.nn.sigmoid(m_b + gate_bias[1]) * o_b
    return x + jnp.einsum('btd,de->bte', merged, w_o)


def _trunk(x, norm_w, w_in, gate_bias, mu_shift, w0, w_up, a0, a_up, k_k, k_a, r_k, ln_w, ln_b,
           w_a_out, conv_w, w_b_out, w_o, final_norm_w):
    for l in range(DEPTH):
        x = _layer(x, norm_w[l], w_in[l], gate_bias[l], mu_shift[l], w0[l], w_up[l], a0[l], a_up[l],
                   k_k[l], k_a[l], r_k[l], ln_w[l], ln_b[l], w_a_out[l], conv_w[l], w_b_out[l], w_o[l])
    return _rmsnorm(x, final_norm_w)


def setup_inputs(seed: int = 0) -> dict:
    key = jax.random.key(seed)
    ks = jax.random.split(key, 20)
    n = lambda i, shape: jax.random.normal(ks[i], shape, jnp.float32)
    L = DEPTH
    return {
        "x_prompt": n(0, (BATCH, SEQ, D_MODEL)),
        "x_sample": n(1, (DEC_BATCH, DEC_SEQ, D_MODEL)),
        "norm_w": 1.0 + 0.02 * n(2, (L, D_MODEL)),
        "w_in": n(3, (L, D_MODEL, N_COLS)) * D_MODEL ** -0.5,
        "gate_bias": 0.02 * n(4, (L, 2, D_MODEL)),
        "mu_shift": 0.5 + 0.1 * n(5, (L, N_SHIFT)),
        "w0": 0.5 * n(6, (L, N_DIR, D_A)),
        "w_up": n(7, (L, N_DIR, LORA_W, D_A)) * 0.3 * LORA_W ** -0.5,
        "a0": 0.3 * n(8, (L, N_DIR, D_A)),
        "a_up": n(9, (L, N_DIR, LORA_A, D_A)) * 0.3 * LORA_A ** -0.5,
        "k_k": 0.85 + 0.05 * n(10, (L, D_A)),
        "k_a": 1.0 + 0.05 * n(11, (L, D_A)),
        "r_k": 0.1 * n(12, (L, N_HEADS_A, HEAD_DIM)),
        "ln_w": 1.0 + 0.02 * n(13, (L, D_A)),
        "ln_b": 0.02 * n(14, (L, D_A)),
        "w_a_out": n(15, (L, D_A, D_MODEL)) * D_A ** -0.5,
        "conv_w": n(16, (L, CONV_W, D_B)) * CONV_W ** -0.5,
        "w_b_out": n(17, (L, D_B, D_MODEL)) * D_B ** -0.5,
        "w_o": n(18, (L, D_MODEL, D_MODEL)) * D_MODEL ** -0.5,
        "final_norm_w": 1.0 + 0.02 * n(19, (D_MODEL,)),
    }


def reference(x_prompt, x_sample, norm_w, w_in, gate_bias, mu_shift, w0, w_up, a0, a_up, k_k, k_a, r_k,
              ln_w, ln_b, w_a_out, conv_w, w_b_out, w_o, final_norm_w):
    y_prompt = _trunk(x_prompt, norm_w, w_in, gate_bias, mu_shift, w0, w_up, a0, a_up, k_k, k_a, r_k,
                      ln_w, ln_b, w_a_out, conv_w, w_b_out, w_o, final_norm_w)
    y_sample = _trunk(x_sample, norm_w, w_in, gate_bias, mu_shift, w0, w_up, a0, a_up, k_k, k_a, r_k,
                      ln_w, ln_b, w_a_out, conv_w, w_b_out, w_o, final_norm_w)
    return (y_prompt, y_sample)
```

```python
import numpy as np
import ml_dtypes
from contextlib import ExitStack
import concourse.bass as bass
import concourse.mybir as mybir
from concourse.bass_utils import run_bass_kernel_spmd

F32 = mybir.dt.float32
BF16 = mybir.dt.bfloat16
AF = mybir.ActivationFunctionType
ALU = mybir.AluOpType

D = 2048
NKC = 16
NCOLS = 12672
BT = 512
NCH = BT // 128
C0 = float(np.exp(-0.5))
RMS_EPS = 1e-6
GN_EPS = 64e-5

PC = {}
_o = 0
for _n, _w in [("w0f", 8), ("w0b", 8), ("a0f", 8), ("a0b", 8), ("kk", 8), ("ka", 8), ("rk", 8), ("lnw", 8), ("lnb", 8),
               ("mur", 8), ("muk", 8), ("muv", 8), ("mul", 4), ("gba", 16), ("gbb", 16), ("cw0", 8), ("cw1", 8), ("cw2", 8),
               ("nw", 16)]:
    PC[_n] = _o
    _o += _w
NPAR = _o


class Ctx:
    def __init__(self, nc, es):
        self.nc = nc
        self.es = es
        self.E = {"pe": nc.tensor, "act": nc.scalar, "dve": nc.vector, "pool": nc.gpsimd, "sp": nc.sync}
        self.sem = {k: es.enter_context(nc.semaphore("sem_" + k)) for k in self.E}
        self.cnt = {k: 0 for k in self.E}
        self.dsem = {}
        self.dcnt = {}
        self.waited = {k: {} for k in self.E}
        self.res = {}

    def _semh(self, key):
        return self.sem[key] if key in self.sem else self.dsem[key]

    def _wait(self, eng, key, val):
        if self.waited[eng].get(key, 0) >= val:
            return
        self.waited[eng][key] = val
        self.E[eng].wait_ge(self._semh(key), val)

    def _deps(self, eng, reads, writes):
        toks = {}

        def add(t):
            if t is not None and toks.get(t[0], 0) < t[1]:
                toks[t[0]] = t[1]

        for r in reads:
            st = self.res.get(r)
            if st:
                add(st[0])
        for w in writes:
            st = self.res.get(w)
            if st:
                add(st[0])
                for k, v in st[1].items():
                    add((k, v))
        for k, v in toks.items():
            if k == eng and eng == "pe":
                continue
            self._wait(eng, k, v)

    def _commit(self, tok, reads, writes):
        for r in reads:
            st = self.res.setdefault(r, [None, {}])
            if st[1].get(tok[0], 0) < tok[1]:
                st[1][tok[0]] = tok[1]
        for w in writes:
            self.res[w] = [tok, {}]

    def op(self, eng, fn, reads=(), writes=()):
        self._deps(eng, reads, writes)
        inst = fn(self.E[eng])
        self.cnt[eng] += 1
        inst.then_inc(self.sem[eng], 1)
        self._commit((eng, self.cnt[eng]), reads, writes)

    def dma(self, q, semkey, out, in_, reads=(), writes=()):
        if semkey not in self.dsem:
            self.dsem[semkey] = self.es.enter_context(self.nc.semaphore("dsem_" + semkey))
            self.dcnt[semkey] = 0
        self._deps(q, reads, writes)
        inst = self.E[q].dma_start(out=out, in_=in_)
        self.dcnt[semkey] += 16
        inst.then_inc(self.dsem[semkey], 16)
        self._commit((semkey, self.dcnt[semkey]), reads, writes)

    def final_wait(self, eng="sp"):
        for k in list(self.sem) + list(self.dsem):
            v = self.cnt[k] if k in self.cnt else self.dcnt[k]
            if v > 0:
                self._wait(eng, k, v)


class _Stop(Exception):
    pass


def build_nc(NT, NBO=None, STOP=0):
    NB = NT // BT
    if NBO is None:
        NBO = NB
    NTO = NBO * BT

    def ck(n):
        if STOP == n:
            raise _Stop()
    nc = bass.Bass("TRN2", target_bir_lowering=False)
    dt = nc.dram_tensor
    x_d = dt("x", [NT, D], F32, kind="ExternalInput").ap()
    wl_d = dt("w_l", [128, NKC * 512], F32, kind="ExternalInput").ap()
    wA_d = dt("w_A", [16, 128, NKC * 256], F32, kind="ExternalInput").ap()
    wB_d = dt("w_B", [16, 128, NKC * 256], F32, kind="ExternalInput").ap()
    wM_d = dt("w_M", [16, 128, NKC * 256], F32, kind="ExternalInput").ap()
    wab_d = dt("w_ab", [8, 128, 2 * 8 * 256], F32, kind="ExternalInput").ap()
    wo_d = dt("w_o", [8, 128, NKC * 256], F32, kind="ExternalInput").ap()
    wup_d = dt("w_up", [128, 4 * 1024], F32, kind="ExternalInput").ap()
    par_d = dt("params", [128, NPAR], F32, kind="ExternalInput").ap()
    fnw_d = dt("fnw", [128, D], F32, kind="ExternalInput").ap()
    flp_d = dt("flagsP", [128, 2 * NB], F32, kind="ExternalInput").ap()
    fl2_d = dt("flags2", [2, NB], F32, kind="ExternalInput").ap()
    cbf_d = dt("cbf", [128, 128 + 2 * 640], BF16, kind="ExternalInput").ap()
    cf_d = dt("cf32", [128, 5 * 128], F32, kind="ExternalInput").ap()
    y_d = dt("y", [NTO, D], F32, kind="ExternalOutput").ap()
    swl = dt("s_wl", [128, NKC * 512], BF16, kind="Internal").ap()
    swA = dt("s_wA", [16, 128, NKC * 256], BF16, kind="Internal").ap()
    swB = dt("s_wB", [16, 128, NKC * 256], BF16, kind="Internal").ap()
    swM = dt("s_wM", [16, 128, NKC * 256], BF16, kind="Internal").ap()
    swab = dt("s_wab", [8, 128, 2 * 8 * 256], BF16, kind="Internal").ap()
    swo = dt("s_wo", [8, 128, NKC * 256], BF16, kind="Internal").ap()
    yscr = dt("s_yf", [8, 128, NTO], F32, kind="Internal").ap()

    es = ExitStack()
    with es:
        K = Ctx(nc, es)

        def sb(name, shape, dtype):
            return es.enter_context(nc.sbuf_tensor("sb_" + name, shape, dtype))

        def ps(name, shape, dtype):
            return es.enter_context(nc.psum_tensor("ps_" + name, shape, dtype))

        cbf = sb("cbf", [128, 128 + 2 * 640], BF16)
        cf = sb("cf", [128, 5 * 128], F32)
        par = sb("par", [128, NPAR], F32)
        par1 = sb("par1", [128, NPAR], F32)
        par2 = sb("par2", [128, NPAR], F32)
        fnw = sb("fnw", [128, D], F32)
        flp = sb("flp", [128, 2 * NB], F32)
        fl2 = sb("fl2", [2, NB], F32)
        wup = sb("wup", [128, 4 * 1024], BF16)
        K.dma("sp", "c0", cbf[:, :], cbf_d[:, :], writes=["cbf"])
        K.dma("sp", "c0", cf[:, :], cf_d[:, :], writes=["cf"])
        K.dma("sp", "c0", par[:, :], par_d[:, :], writes=["par"])
        K.dma("sp", "c0", fnw[:, :], fnw_d[:, :], writes=["fnw"])
        K.dma("sp", "c0", flp[:, :], flp_d[:, :], writes=["flp"])
        K.dma("sp", "c0", fl2[:, :], fl2_d[:, :], writes=["fl2"])
        for _nm in ["cbf", "cf", "par", "fnw", "flp", "fl2"]:
            K.res[_nm][0] = ("c0", K.dcnt["c0"])
        ident = cbf[:, 0:128]

        def masks(d):
            o = 128 + d * 640
            return cbf[:, o:o + 256], cbf[:, o + 256:o + 512], cbf[:, o + 512:o + 640]

        blockones = cf[:, 0:128]
        blockavg = cf[:, 128:256]
        sel0 = cf[0:64, 256:384]
        sel1 = cf[0:64, 384:512]
        ones128 = cf[:, 512:640]
        K.op("dve", lambda e: e.tensor_scalar(out=par1[:, :], in0=par[:, :], scalar1=-1.0, scalar2=1.0,
                                              op0=ALU.mult, op1=ALU.add), reads=["par"], writes=["par1"])
        K.op("dve", lambda e: e.tensor_scalar(out=par2[:, :], in0=par[:, :], scalar1=0.5, scalar2=None,
                                              op0=ALU.mult), reads=["par"], writes=["par2"])

        def pcol(t, name, i, np_=128):
            return t[0:np_, PC[name] + i:PC[name] + i + 1]

        with nc.sbuf_tensor("pst_f", [128, 2, 4096], F32) as stg_f, nc.sbuf_tensor("pst_b", [128, 2, 4096], BF16) as stg_b:
            jobs = [(wl_d[:, :], swl[:, :], NKC * 512)]
            for src, dst, n, F in [(wA_d, swA, 16, NKC * 256), (wB_d, swB, 16, NKC * 256), (wM_d, swM, 16, NKC * 256),
                                   (wab_d, swab, 8, 4096), (wo_d, swo, 8, NKC * 256)]:
                for i in range(n):
                    jobs.append((src[i, :, :], dst[i, :, :], F))
            ci = 0
            for (src, dst, F) in jobs:
                for c0 in range(0, F, 4096):
                    w = min(4096, F - c0)
                    s = ci % 2
                    K.dma("sp", "pcl%d" % s, stg_f[:, s, 0:w], src[:, c0:c0 + w], writes=["stg_f%d" % s])
                    eng = ["dve", "act", "pool"][ci % 3]
                    if eng == "act":
                        K.op("act", lambda e, s=s, w=w: e.activation(out=stg_b[:, s, 0:w], in_=stg_f[:, s, 0:w], func=AF.Copy),
                             reads=["stg_f%d" % s], writes=["stg_b%d" % s])
                    else:
                        K.op(eng, lambda e, s=s, w=w: e.tensor_copy(out=stg_b[:, s, 0:w], in_=stg_f[:, s, 0:w]),
                             reads=["stg_f%d" % s], writes=["stg_b%d" % s])
                    K.dma("pool", "pcs%d" % s, dst[:, c0:c0 + w], stg_b[:, s, 0:w], reads=["stg_b%d" % s], writes=["wscr"])
                    ci += 1
            K.dma("sp", "pcl0", stg_f[:, 0, :], wup_d[:, :], writes=["stg_f0"])
            K.op("dve", lambda e: e.tensor_copy(out=wup[:, :], in_=stg_f[:, 0, :]), reads=["stg_f0"], writes=["wup"])
            for eng in ["pe", "act", "dve", "pool", "sp"]:
                K.final_wait(eng)

        NS = 18
        pool = sb("pool", [128, NS, BT], F32)
        xt = sb("xt", [128, D], F32)
        xs = sb("xs", [128, D], BF16)
        xh = sb("xh", [2, D], F32)
        xhs = sb("xhs", [2, D], BF16)
        st = sb("stats", [128, 8], F32)
        hT = sb("hT", [128, NKC, BT], BF16)
        hTh = sb("hTh", [128, NKC, 2], BF16)
        wbf = [sb("wbf%d" % i, [128, NKC * 512], BF16) for i in range(2)]
        wabt = sb("wab", [128, 2, 8, 256], BF16)
        twd = sb("twd", [128, BT], BF16)
        adb = sb("adb", [128, BT], BF16)
        zf = sb("zf", [128, BT + 2], F32)
        KR = sb("KR", [128, NCH, 2, 128], BF16)
        R2 = sb("R2", [128, NCH, 2, 128], BF16)
        BTt = sb("BTt", [128, BT], BF16)
        KTt = sb("KTt", [128, BT], BF16)
        vbf = sb("vbf", [128, BT], BF16)
        Vt = sb("Vt", [128, NCH, 128], BF16)
        BKt = sb("BKt", [128, NCH, 2, 128], BF16)
        MAt = sb("MAt", [128, 2 * NCH, 256], BF16)
        AKt = sb("AKt", [128, 2 * NCH, 256], BF16)
        XTt = sb("XTt", [128, 2 * NCH, 128], BF16)
        Nt = sb("Nt", [128, 2, 128], BF16)
        Mt = sb("Mt", [128, 2, 128], BF16)
        XTw = sb("XTw", [128, 2, 128], BF16)
        Xn = sb("Xn", [128, 128], BF16)
        Ub = sb("Ub", [128, 128], BF16)
        Hf = sb("Hf", [128, 8, 64], F32)
        Hb = sb("Hb", [128, 8, 64], BF16)
        H0P = sb("H0P", [128, 64], F32)
        ysb = sb("ysb", [64, 2, BT], F32)
        yg = sb("yg", [128, 8, BT], BF16)
        ybg = sb("ybg", [128, 8, BT], BF16)
        merged = sb("merged", [128, NKC, BT], BF16)
        uf = sb("uf", [128, BT + 2], F32)
        hal = sb("hal", [128, 4], F32)

        pbank = [ps("pb%d" % i, [128, 512], F32) for i in range(5)]
        ptr = ps("ptr", [128, 1024], BF16)
        yps = ps("yps", [64, NCH, 2, 128], F32)

        def S(i):
            return pool[:, i, :]

        def SN(i):
            return "pool%d" % i

        K.op("pool", lambda e: e.memset(R2[:, :, :, :].rearrange("p a b c -> p (a b c)"), 0.0), writes=["R2"])

        wq = []
        wstate = {"issued": 0, "used": 0}

        def w_issue():
            i = wstate["issued"]
            s = i % 2
            src, ncol = wq[i]
            K.dma("sp", "wl%d" % s, wbf[s][:, 0:NKC * ncol], src, reads=["wscr"], writes=["wbf%d" % s])
            wstate["issued"] += 1

        def w_next():
            i = wstate["used"]
            while wstate["issued"] <= min(i + 1, len(wq) - 1):
                w_issue()
            wstate["used"] += 1
            ncol = wq[i][1]
            return wbf[i % 2][:, 0:NKC * ncol].rearrange("p (k n) -> p k n", n=ncol), "wbf%d" % (i % 2)

        for pas in range(2):
            for bi in (list(range(NBO)) if pas == 0 else list(range(NB - 1, -1, -1))):
                wq.append((swl[:, :], 512))
                for p in range(16):
                    wq.append((swA[p, :, :], 256))
                if pas == 1 and bi < NBO:
                    for q in range(16):
                        wq.append((swB[q, :, :], 256))
                    for j in range(16):
                        wq.append((swM[j, :, :], 256))
                    for n in range(8):
                        wq.append((swo[n, :, :], 256))

        hps_col = [0]

        def inproj(wt, wname, col0, M, pmain, pmain_name, halo):
            hap = None
            if halo:
                c = hps_col[0] % 64
                hps_col[0] += 1
                hap = pbank[2][0:M, 4 * c:4 * c + 2]
            for kc in range(NKC):
                K.op("pe", lambda e, kc=kc: e.matmul(pmain, wt[:, kc, col0:col0 + M], hT[:, kc, :], start=(kc == 0), stop=(kc == NKC - 1)),
                     reads=[wname, "hT"], writes=[pmain_name])
            ck(22)
            if halo:
                for kc in range(NKC):
                    K.op("pe", lambda e, kc=kc: e.matmul(hap, wt[:, kc, col0:col0 + M], hTh[:, kc, :], start=(kc == 0), stop=(kc == NKC - 1)),
                         reads=[wname, "hTh"], writes=["pb2"])
            ck(23)
            return hap

        def shiftmix(M, pmain, pmain_name, hap, c1, c2, out_i):
            K.op("act", lambda e: e.activation(out=zf[0:M, 1:BT + 1], in_=pmain, func=AF.Copy), reads=[pmain_name], writes=["zf"])
            K.op("act", lambda e: e.activation(out=zf[0:M, 0:1], in_=hap[:, 0:1], func=AF.Copy), reads=["pb2"], writes=["zf"])
            K.op("act", lambda e: e.activation(out=zf[0:M, BT + 1:BT + 2], in_=hap[:, 1:2], func=AF.Copy), reads=["pb2"], writes=["zf"])
            ck(24)
            K.op("dve", lambda e: e.tensor_scalar(out=S(0)[0:M, :], in0=zf[0:M, 1:BT + 1], scalar1=c1, scalar2=None, op0=ALU.mult),
                 reads=["zf", "par1"], writes=[SN(0)])
            ck(25)
            K.op("dve", lambda e: e.tensor_tensor(out=S(1)[0:M, :], in0=zf[0:M, 0:BT], in1=zf[0:M, 2:BT + 2], op=ALU.add),
                 reads=["zf"], writes=[SN(1)])
            K.op("dve", lambda e: e.scalar_tensor_tensor(out=S(out_i)[0:M, :], in0=S(1)[0:M, :], scalar=c2, in1=S(0)[0:M, :],
                                                         op0=ALU.mult, op1=ALU.add),
                 reads=[SN(0), SN(1), "par2"], writes=[SN(out_i)])

        def rstd_rows(src, srcnames, npart, c_ss, c_out, flagcol=None):
            K.op("pool", lambda e: e.memset(st[0:npart, c_ss:c_ss + 1], 0.0), writes=["st"])
            K.op("act", lambda e: e.activation(out=xs[0:npart, :], in_=src, func=AF.Square, accum_out=st[0:npart, c_ss:c_ss + 1]),
                 reads=list(srcnames) + ["st"], writes=["xs", "st"])
            K.op("dve", lambda e: e.tensor_scalar(out=st[0:npart, c_out:c_out + 1], in0=st[0:npart, c_ss:c_ss + 1],
                                                  scalar1=1.0 / D, scalar2=RMS_EPS, op0=ALU.mult, op1=ALU.add),
                 reads=["st"], writes=["st"])
            K.op("act", lambda e: e.activation(out=st[0:npart, c_out:c_out + 1], in_=st[0:npart, c_out:c_out + 1], func=AF.Sqrt),
                 reads=["st"], writes=["st"])
            K.op("dve", lambda e: e.reciprocal(out=st[0:npart, c_out:c_out + 1], in_=st[0:npart, c_out:c_out + 1]),
                 reads=["st"], writes=["st"])
            if flagcol is not None:
                K.op("dve", lambda e: e.tensor_scalar(out=st[0:npart, c_out:c_out + 1], in0=st[0:npart, c_out:c_out + 1],
                                                      scalar1=flagcol, scalar2=None, op0=ALU.mult),
                     reads=["st", "fl2"], writes=["st"])

        def build_hT(b):
            t0 = b * BT
            for i in range(BT // 128):
                K.dma("sp", "xl", xt[:, :], x_d[t0 + i * 128:t0 + (i + 1) * 128, :], writes=["xt"])
                rstd_rows(xt[:, :], ["xt"], 128, 0, 1)
                K.op("dve", lambda e: e.tensor_scalar(out=xs[:, :], in0=xt[:, :], scalar1=st[:, 1:2], scalar2=None, op0=ALU.mult),
                     reads=["xt", "st"], writes=["xs"])
                for half in range(2):
                    for k8 in range(8):
                        kc = half * 8 + k8
                        K.op("pe", lambda e, kc=kc, k8=k8: e.transpose(ptr[:, k8 * 128:(k8 + 1) * 128], xs[:, kc * 128:(kc + 1) * 128], ident),
                             reads=["xs", "cbf"], writes=["ptr"])
                    for k8 in range(8):
                        kc = half * 8 + k8
                        K.op("dve", lambda e, kc=kc, k8=k8, i=i: e.tensor_scalar(out=hT[:, kc, i * 128:(i + 1) * 128],
                                                                                 in0=ptr[:, k8 * 128:(k8 + 1) * 128],
                                                                                 scalar1=pcol(par, "nw", kc), scalar2=None, op0=ALU.mult),
                             reads=["ptr", "par"], writes=["hT"])
            tl = max(t0 - 1, 0)
            tr = min(t0 + BT, NT - 1)
            K.dma("sp", "xhl", xh[0:1, :], x_d[tl:tl + 1, :], writes=["xh"])
            K.dma("sp", "xhl", xh[1:2, :], x_d[tr:tr + 1, :], writes=["xh"])
            rstd_rows(xh[0:2, :], ["xh"], 2, 2, 3, flagcol=fl2[0:2, b:b + 1])
            K.op("dve", lambda e: e.tensor_scalar(out=xhs[0:2, :], in0=xh[0:2, :], scalar1=st[0:2, 3:4], scalar2=None, op0=ALU.mult),
                 reads=["xh", "st"], writes=["xhs"])
            for kc in range(NKC):
                K.op("pe", lambda e, kc=kc: e.transpose(ptr[:, 2 * kc:2 * kc + 2], xhs[0:2, kc * 128:(kc + 1) * 128], ident[0:2, 0:2]),
                     reads=["xhs", "cbf"], writes=["ptr"])
            for kc in range(NKC):
                K.op("dve", lambda e, kc=kc: e.tensor_scalar(out=hTh[:, kc, :], in0=ptr[:, 2 * kc:2 * kc + 2], scalar1=pcol(par, "nw", kc),
                                                             scalar2=None, op0=ALU.mult),
                     reads=["ptr", "par"], writes=["hTh"])

        def c3(ap):
            return ap.rearrange("p (c t) -> p c t", t=128)

        def f2(ap):
            return ap.rearrange("p a t -> p (a t)")

        def tt(eng, out, in0, in1, op, reads, writes):
            K.op(eng, lambda e: e.tensor_tensor(out=out, in0=in0, in1=in1, op=op), reads=reads, writes=writes)

        def act(out, in_, func, reads, writes, **kw):
            K.op("act", lambda e: e.activation(out=out, in_=in_, func=func, **kw), reads=reads, writes=writes)

        def mm(out, lhsT, rhs, start, stop, reads, writes):
            K.op("pe", lambda e: e.matmul(out, lhsT, rhs, start=start, stop=stop), reads=reads, writes=writes)

        pb0, pb1, pb2, pb3, pb4 = pbank[0][:, :], pbank[1][:, :], pbank[2], pbank[3], pbank[4]
        Xps = pb2[:, 256:384]
        Ups = pb2[:, 384:512]
        ps1 = pb3[:, 0:256]
        ps2 = pb2[:, 0:256]
        ps3 = pb4[:, 0:128]
        sqN = pb4[:, 128:256]
        sqM = pbank[0][:, 0:128]
        xu = pbank[1][:, 0:128]

        try:
            for pas in range(2):
                d = pas
                mMA, mAK, mN = masks(d)
                K.op("pool", lambda e: e.memset(Hf[:, :, :], 0.0), writes=["Hf"])
                K.op("pool", lambda e: e.memset(Hb[:, :, :], 0.0), writes=["Hb"])
                blocks = list(range(NBO)) if pas == 0 else list(range(NB - 1, -1, -1))
                for b in blocks:
                    so = (pas == 1 and b >= NBO)
                    t0 = b * BT
                    ck(1)
                    build_hT(b)
                    fcol = flp[:, 2 * b + d:2 * b + d + 1]
                    K.op("dve", lambda e: e.tensor_scalar(out=Hf[:, :, :], in0=Hf[:, :, :], scalar1=fcol, scalar2=None, op0=ALU.mult),
                         reads=["Hf", "flp"], writes=["Hf"])
                    K.op("pool", lambda e: e.tensor_copy(out=Hb[:, :, :], in_=Hf[:, :, :]), reads=["Hf"], writes=["Hb"])
                    ck(21)
                    ck(2)
                    wt, wn = w_next()
                    for li, (cc, dst, fn) in enumerate([(d * 128, twd, AF.Tanh), (256 + d * 128, adb, AF.Copy)]):
                        pm = pbank[li][:, :]
                        hap = inproj(wt, wn, cc, 128, pm, "pb%d" % li, True)
                        mi = (0 if li == 0 else 2) + d
                        shiftmix(128, pm, "pb%d" % li, hap, pcol(par1, "mul", mi), pcol(par2, "mul", mi), 2)
                        if li == 0:
                            act(S(2), S(2), AF.Sigmoid, [SN(2)], [SN(2)], scale=2.0)
                            K.op("dve", lambda e: e.tensor_scalar(out=twd[:, :], in0=S(2), scalar1=2.0, scalar2=-1.0,
                                                                  op0=ALU.mult, op1=ALU.add), reads=[SN(2)], writes=["lora0"])
                        else:
                            K.op("dve", lambda e: e.tensor_copy(out=adb[:, :], in_=S(2)), reads=[SN(2)], writes=["lora1"])
                    for p in range(8):
                        ck(3)
                        wt, wn = w_next()
                        if not so:
                            hap = inproj(wt, wn, 0, 128, pb0, "pb0", True)
                            shiftmix(128, pb0, "pb0", hap, pcol(par1, "mur", p), pcol(par2, "mur", p), 3)
                        hap = inproj(wt, wn, 128, 128, pb1, "pb1", True)
                        shiftmix(128, pb1, "pb1", hap, pcol(par1, "muk", p), pcol(par2, "muk", p), 4)
                        wt, wn = w_next()
                        hap = inproj(wt, wn, 0, 128, pb0, "pb0", True)
                        shiftmix(128, pb0, "pb0", hap, pcol(par1, "muv", p), pcol(par2, "muv", p), 5)
                        if pas == 1 and not so:
                            inproj(wt, wn, 128, 128, pb1, "pb1", False)
                            act(S(17), pb1, AF.Sigmoid, ["pb1"], [SN(17)])
                            tt("dve", S(17), S(17), pb1, ALU.mult, [SN(17), "pb1"], [SN(17)])
                        ck(4)
                        mm(pb0, wup[:, d * 1024 + p * 128:d * 1024 + (p + 1) * 128], twd[:, :], True, True, ["wup", "lora0"], ["pb0"])
                        act(S(6), pb0, AF.Sigmoid, ["pb0", "par"], [SN(6)], bias=pcol(par, "w0b" if d else "w0f", p))
                        mm(pb1, wup[:, (2 + d) * 1024 + p * 128:(2 + d) * 1024 + (p + 1) * 128], adb[:, :], True, True, ["wup", "lora1"], ["pb1"])
                        act(S(7), pb1, AF.Sigmoid, ["pb1", "par"], [SN(7)], bias=pcol(par, "a0b" if d else "a0f", p))
                        K.op("dve", lambda e: e.tensor_scalar(out=S(8), in0=S(4), scalar1=pcol(par, "kk", p), scalar2=None, op0=ALU.mult),
                             reads=[SN(4), "par"], writes=[SN(8)])
                        tt("dve", S(0), S(8), S(8), ALU.mult, [SN(8)], [SN(0)])
                        mm(pb0, blockones, S(0), True, True, ["cf", SN(0)], ["pb0"])
                        act(S(9), pb0, AF.Sqrt, ["pb0"], [SN(9)])
                        K.op("dve", lambda e: e.tensor_scalar(out=S(9), in0=S(9), scalar1=1e-12, scalar2=None, op0=ALU.max),
                             reads=[SN(9)], writes=[SN(9)])
                        K.op("dve", lambda e: e.reciprocal(out=S(9), in_=S(9)), reads=[SN(9)], writes=[SN(9)])
                        tt("dve", S(8), S(8), S(9), ALU.mult, [SN(8), SN(9)], [SN(8)])
                        K.op("dve", lambda e: e.tensor_scalar(out=S(10), in0=S(7), scalar1=-1.0, scalar2=pcol(par, "ka", p),
                                                              op0=ALU.add, op1=ALU.mult), reads=[SN(7), "par"], writes=[SN(10)])
                        K.op("dve", lambda e: e.scalar_tensor_tensor(out=S(10), in0=S(10), scalar=1.0, in1=S(4), op0=ALU.add, op1=ALU.mult),
                             reads=[SN(10), SN(4)], writes=[SN(10)])
                        tt("dve", S(7), S(8), S(7), ALU.mult, [SN(8), SN(7)], [SN(7)])
                        for c in range(NCH):
                            cs = slice(c * 128, (c + 1) * 128)
                            K.op("dve", lambda e, cs=cs: e.tensor_tensor_scan(out=S(11)[:, cs], data0=ones128, data1=S(6)[:, cs], initial=0.0,
                                                                              op0=ALU.mult, op1=ALU.add), reads=["cf", SN(6)], writes=[SN(11)])
                        if d == 1:
                            for c in range(NCH):
                                cs = slice(c * 128, (c + 1) * 128)
                                K.op("dve", lambda e, cs=cs, c=c: e.tensor_scalar(out=S(12)[:, cs], in0=S(11)[:, cs],
                                                                                  scalar1=S(11)[:, c * 128 + 127:c * 128 + 128], scalar2=-1.0,
                                                                                  op0=ALU.subtract, op1=ALU.mult), reads=[SN(11)], writes=[SN(12)])
                            tt("dve", S(11), S(12), S(6), ALU.add, [SN(12), SN(6)], [SN(11)])
                        tt("dve", S(12), S(11), S(6), ALU.subtract, [SN(11), SN(6)], [SN(12)])
                        act(S(13), S(11), AF.Exp, [SN(11)], [SN(13)], scale=-C0)
                        act(S(14), S(11), AF.Exp, [SN(11)], [SN(14)], scale=C0)
                        act(S(15), S(12), AF.Exp, [SN(12)], [SN(15)], scale=-C0)
                        tt("dve", KR[:, :, 0, :], c3(S(8)), c3(S(15)), ALU.mult, [SN(8), SN(15)], ["KR"])
                        if not so:
                            tt("dve", KR[:, :, 1, :], c3(S(3)), c3(S(13)), ALU.mult, [SN(3), SN(13)], ["KR"])
                            tt("dve", R2[0:64, :, 0, :], c3(S(3)[0:64, :]), c3(S(13)[0:64, :]), ALU.mult, [SN(3), SN(13)], ["R2"])
                            tt("dve", R2[64:128, :, 1, :], c3(S(3)[64:128, :]), c3(S(13)[64:128, :]), ALU.mult, [SN(3), SN(13)], ["R2"])
                        tt("dve", BTt[:, :], S(7), S(14), ALU.mult, [SN(7), SN(14)], ["BTt"])
                        tt("dve", KTt[:, :], S(10), S(14), ALU.mult, [SN(10), SN(14)], ["KTt"])
                        if not so:
                            K.op("dve", lambda e: e.scalar_tensor_tensor(out=S(0), in0=S(3), scalar=pcol(par, "rk", p), in1=S(10),
                                                                         op0=ALU.mult, op1=ALU.mult), reads=[SN(3), SN(10), "par"], writes=[SN(0)])
                            mm(pb1, blockones, S(0), True, True, ["cf", SN(0)], ["pb1"])
                            tt("dve", S(16), pb1, S(5), ALU.mult, ["pb1", SN(5)], [SN(16)])
                        act(vbf[:, :], S(5), AF.Copy, [SN(5)], ["vbf"])
                        for c in range(NCH):
                            K.op("pe", lambda e, c=c: e.transpose(ptr[:, c * 128:(c + 1) * 128], vbf[:, c * 128:(c + 1) * 128], ident),
                                 reads=["vbf", "cbf"], writes=["ptr"])
                        act(f2(Vt[:, :, :]), ptr[:, 0:512], AF.Copy, ["ptr"], ["Vt"])
                        for c in range(NCH):
                            K.op("pe", lambda e, c=c: e.transpose(ptr[:, 512 + c * 128:512 + (c + 1) * 128], BTt[:, c * 128:(c + 1) * 128], ident),
                                 reads=["BTt", "cbf"], writes=["ptr"])
                        K.op("dve", lambda e: e.tensor_copy(out=BKt[:, :, 0, :], in_=c3(ptr[:, 512:1024])), reads=["ptr"], writes=["BKt"])
                        for c in range(NCH):
                            K.op("pe", lambda e, c=c: e.transpose(ptr[:, c * 128:(c + 1) * 128], KTt[:, c * 128:(c + 1) * 128], ident),
                                 reads=["KTt", "cbf"], writes=["ptr"])
                        act(BKt[:, :, 1, :], c3(ptr[:, 0:512]), AF.Copy, ["ptr"], ["BKt"])
                        ck(5)
                        for h in range(2):
                            hp = slice(64 * h, 64 * h + 64)
                            for c in range(NCH):
                                hc = h * NCH + c
                                cs = slice(c * 128, (c + 1) * 128)
                                KRh = f2(KR[hp, c, :, :])
                                nw_ = 128 if so else 256
                                if so:
                                    KRh = KR[hp, c, 0, :]
                                mm(ps1[:, 0:nw_], BTt[hp, cs], KRh, True, True, ["BTt", "KR"], ["pb3"])
                                mm(ps2[:, 0:nw_], KTt[hp, cs], KRh, True, True, ["KTt", "KR"], ["pb2"])
                                mm(ps3, KR[hp, c, 0, :], BTt[hp, cs], True, True, ["BTt", "KR"], ["pb4"])
                                tt("dve", MAt[:, hc, 0:nw_], ps1[:, 0:nw_], mMA[:, 0:nw_], ALU.mult, ["pb3", "cbf"], ["MAt%d" % hc])
                                tt("dve", AKt[:, hc, 0:nw_], ps2[:, 0:nw_], mAK[:, 0:nw_], ALU.mult, ["pb2", "cbf"], ["AKt%d" % hc])
                                tt("dve", Nt[:, 0, :], ps3, mN, ALU.mult, ["pb4", "cbf"], ["Nt0"])
                                tt("dve", XTw[:, 0, :], MAt[:, hc, 0:128], ident, ALU.add, ["MAt%d" % hc, "cbf"], ["XTw0"])
                                Ncur, Nname = Nt[:, 0, :], "Nt0"
                                Mcur, Mname = MAt[:, hc, 0:128], "MAt%d" % hc
                                Xcur, Xname = XTw[:, 0, :], "XTw0"
                                for j in range(1, 7):
                                    s = j % 2
                                    mm(sqN, Mcur, Ncur, True, True, [Mname, Nname], ["pb4"])
                                    if j < 6:
                                        mm(sqM, Ncur, Mcur, True, True, [Mname, Nname], ["pb0"])
                                    act(Nt[:, s, :], sqN, AF.Copy, ["pb4"], ["Nt%d" % s])
                                    if j < 6:
                                        K.op("dve", lambda e, s=s: e.tensor_copy(out=Mt[:, s, :], in_=sqM), reads=["pb0"], writes=["Mt%d" % s])
                                    Ncur, Nname = Nt[:, s, :], "Nt%d" % s
                                    if j < 6:
                                        Mcur, Mname = Mt[:, s, :], "Mt%d" % s
                                    mm(xu, Ncur, Xcur, True, True, [Nname, Xname], ["pb1"])
                                    if j < 6:
                                        Xnew, Xnn = XTw[:, s, :], "XTw%d" % s
                                    else:
                                        Xnew, Xnn = XTt[:, hc, :], "XTt%d" % hc
                                    tt("dve", Xnew, xu, Xcur, ALU.add, ["pb1", Xname], [Xnn])
                                    Xcur, Xname = Xnew, Xnn
                        ck(6)
                        order = list(range(NCH)) if d == 0 else list(range(NCH - 1, -1, -1))
                        for c in order:
                            pidx = c * 128 + (127 if d == 0 else 0)
                            ptot = S(13)[:, pidx:pidx + 1]
                            K.op("dve", lambda e, ptot=ptot: e.tensor_scalar(out=H0P[:, :], in0=Hf[:, p, :], scalar1=ptot, scalar2=None, op0=ALU.mult),
                                 reads=["Hf", SN(13)], writes=["H0P"])
                            for h in range(2):
                                hp = slice(64 * h, 64 * h + 64)
                                hc = h * NCH + c
                                hs = slice(64 * h, 64 * h + 64)
                                mm(Xps[:, hs], AKt[:, hc, 0:128], Vt[:, c, hs], True, False, ["AKt%d" % hc, "Vt"], ["pb2"])
                                mm(Xps[:, hs], KR[hp, c, 0, :], Hb[hp, p, :], False, True, ["KR", "Hb"], ["pb2"])
                            act(Xn[:, :], Xps, AF.Copy, ["pb2"], ["Xn"], scale=-1.0)
                            for h in range(2):
                                hc = h * NCH + c
                                hs = slice(64 * h, 64 * h + 64)
                                mm(Ups[:, hs], XTt[:, hc, :], Xn[:, hs], True, True, ["XTt%d" % hc, "Xn"], ["pb2"])
                            K.op("dve", lambda e: e.tensor_copy(out=Ub[:, :], in_=Ups), reads=["pb2"], writes=["Ub"])
                            for h in range(0 if so else 2):
                                hc = h * NCH + c
                                hs = slice(64 * h, 64 * h + 64)
                                mm(yps[:, c, h, :], Hb[:, p, :], R2[:, c, h, :], True, False, ["Hb", "R2"], ["yps%d" % (c // 2)])
                                mm(yps[:, c, h, :], Ub[:, hs], MAt[:, hc, 128:256], False, False, ["Ub", "MAt%d" % hc], ["yps%d" % (c // 2)])
                                mm(yps[:, c, h, :], Vt[:, c, hs], AKt[:, hc, 128:256], False, True, ["Vt", "AKt%d" % hc], ["yps%d" % (c // 2)])
                            mm(pb1[:, 0:128], BKt[:, c, 0, :], Ub[:, :], True, False, ["BKt", "Ub"], ["pb1"])
                            mm(pb1[:, 0:128], BKt[:, c, 1, :], Vt[:, c, :], False, True, ["BKt", "Vt"], ["pb1"])
                            for h in range(2):
                                hp = slice(64 * h, 64 * h + 64)
                                K.op("dve", lambda e, hp=hp, h=h, ptot=ptot: e.scalar_tensor_tensor(
                                    out=Hf[hp, p, :], in0=pbank[1][hp, 64 * h:64 * h + 64], scalar=ptot[hp, :], in1=H0P[hp, :],
                                    op0=ALU.mult, op1=ALU.add), reads=["pb1", "H0P", SN(13)], writes=["Hf"])
                            act(Hb[:, p, :], Hf[:, p, :], AF.Copy, ["Hf"], ["Hb"])
                        ck(7)
                        if so:
                            continue
                        for h in range(2):
                            act(c3(ysb[:, h, :]), yps[:, :, h, :], AF.Copy, ["yps0", "yps1"], ["ysb"])
                        mm(pb0, sel0, ysb[:, 0, :], True, False, ["cf", "ysb"], ["pb0"])
                        mm(pb0, sel1, ysb[:, 1, :], False, True, ["cf", "ysb"], ["pb0"])
                        act(S(3), pb0, AF.Copy, ["pb0"], [SN(3)])
                        mm(pb1, blockavg, S(3), True, True, ["cf", SN(3)], ["pb1"])
                        tt("dve", S(4), S(3), pb1, ALU.subtract, [SN(3), "pb1"], [SN(4)])
                        tt("dve", S(5), S(4), S(4), ALU.mult, [SN(4)], [SN(5)])
                        mm(pb0, blockavg, S(5), True, True, ["cf", SN(5)], ["pb0"])
                        K.op("dve", lambda e: e.tensor_scalar(out=S(6), in0=pb0, scalar1=GN_EPS, scalar2=None, op0=ALU.add),
                             reads=["pb0"], writes=[SN(6)])
                        act(S(6), S(6), AF.Sqrt, [SN(6)], [SN(6)])
                        K.op("dve", lambda e: e.reciprocal(out=S(6), in_=S(6)), reads=[SN(6)], writes=[SN(6)])
                        tt("dve", S(7), S(4), S(6), ALU.mult, [SN(4), SN(6)], [SN(7)])
                        K.op("dve", lambda e: e.tensor_scalar(out=S(7), in0=S(7), scalar1=pcol(par, "lnw", p), scalar2=pcol(par, "lnb", p),
                                                              op0=ALU.mult, op1=ALU.add), reads=[SN(7), "par"], writes=[SN(7)])
                        tt("dve", S(7), S(7), S(16), ALU.add, [SN(7), SN(16)], [SN(7)])
                        if pas == 0:
                            K.dma("pool", "yst", yscr[p, :, t0:t0 + BT], S(7), reads=[SN(7)], writes=["yscr"])
                        else:
                            K.dma("pool", "yld", S(8), yscr[p, :, t0:t0 + BT], reads=["yscr"], writes=[SN(8)])
                            tt("dve", S(7), S(7), S(8), ALU.add, [SN(7), SN(8)], [SN(7)])
                            tt("dve", yg[:, p, :], S(7), S(17), ALU.mult, [SN(7), SN(17)], ["yg"])
                    if pas == 0 or so:
                        continue
                    for q in range(8):
                        wt, wn = w_next()
                        inproj(wt, wn, 0, 128, pb0, "pb0", False)
                        act(S(3), pb0, AF.Copy, ["pb0"], [SN(3)])
                        hapC = inproj(wt, wn, 128, 128, pb1, "pb1", True)
                        act(S(4), pb1, AF.Copy, ["pb1"], [SN(4)])
                        act(hal[:, 0:2], hapC, AF.Copy, ["pb2"], ["hal"])
                        wt, wn = w_next()
                        hapH = inproj(wt, wn, 0, 128, pb0, "pb0", True)
                        tt("dve", uf[:, 1:BT + 1], S(4), pb0, ALU.mult, [SN(4), "pb0"], ["uf"])
                        tt("dve", uf[:, 0:1], hal[:, 0:1], hapH[:, 0:1], ALU.mult, ["hal", "pb2"], ["uf"])
                        tt("dve", uf[:, BT + 1:BT + 2], hal[:, 1:2], hapH[:, 1:2], ALU.mult, ["hal", "pb2"], ["uf"])
                        inproj(wt, wn, 128, 128, pb1, "pb1", False)
                        act(S(7), pb1, AF.Sigmoid, ["pb1"], [SN(7)])
                        tt("dve", S(7), S(7), pb1, ALU.mult, [SN(7), "pb1"], [SN(7)])
                        K.op("dve", lambda e: e.tensor_scalar(out=S(5), in0=uf[:, 0:BT], scalar1=pcol(par, "cw0", q), scalar2=None, op0=ALU.mult),
                             reads=["uf", "par"], writes=[SN(5)])
                        K.op("dve", lambda e: e.scalar_tensor_tensor(out=S(5), in0=uf[:, 1:BT + 1], scalar=pcol(par, "cw1", q), in1=S(5),
                                                                     op0=ALU.mult, op1=ALU.add), reads=["uf", "par", SN(5)], writes=[SN(5)])
                        K.op("dve", lambda e: e.scalar_tensor_tensor(out=S(5), in0=uf[:, 2:BT + 2], scalar=pcol(par, "cw2", q), in1=S(5),
                                                                     op0=ALU.mult, op1=ALU.add), reads=["uf", "par", SN(5)], writes=[SN(5)])
                        tt("dve", S(6), S(3), S(5), ALU.mult, [SN(3), SN(5)], [SN(6)])
                        tt("dve", ybg[:, q, :], S(6), S(7), ALU.mult, [SN(6), SN(7)], ["ybg"])
                    for j in range(8):
                        K.dma("sp", "wab", f2(wabt[:, :, :, :].rearrange("p a k n -> p a (k n)")), swab[j, :, :], reads=["wscr"], writes=["wab"])
                        wt, wn = w_next()
                        for g2 in range(2):
                            inproj(wt, wn, g2 * 128, 128, pbank[g2][:, :], "pb%d" % g2, False)
                            act(S(3 + g2), pbank[g2][:, :], AF.Sigmoid, ["pb%d" % g2, "par"], [SN(3 + g2)], bias=pcol(par, "gba", 2 * j + g2))
                        wt, wn = w_next()
                        for g2 in range(2):
                            inproj(wt, wn, g2 * 128, 128, pbank[g2][:, :], "pb%d" % g2, False)
                            act(S(5 + g2), pbank[g2][:, :], AF.Sigmoid, ["pb%d" % g2, "par"], [SN(5 + g2)], bias=pcol(par, "gbb", 2 * j + g2))
                        for g2 in range(2):
                            for kc in range(8):
                                mm(pb0, wabt[:, 0, kc, g2 * 128:(g2 + 1) * 128], yg[:, kc, :], kc == 0, kc == 7, ["wab", "yg"], ["pb0"])
                            tt("dve", S(7), pb0, S(3 + g2), ALU.mult, ["pb0", SN(3 + g2)], [SN(7)])
                            for kc in range(8):
                                mm(pb1, wabt[:, 1, kc, g2 * 128:(g2 + 1) * 128], ybg[:, kc, :], kc == 0, kc == 7, ["wab", "ybg"], ["pb1"])
                            tt("dve", S(8), pb1, S(5 + g2), ALU.mult, ["pb1", SN(5 + g2)], [SN(8)])
                            tt("dve", merged[:, 2 * j + g2, :], S(7), S(8), ALU.add, [SN(7), SN(8)], ["merged"])
                    def res(t_):
                        return pool[:, 4 * t_:4 * t_ + 4, :].rearrange("p s t -> p (s t)")

                    def resn(t_):
                        return [SN(4 * t_ + k) for k in range(4)]

                    for n in range(8):
                        wt, wn = w_next()
                        for t_ in range(4):
                            pbx = pbank[t_ % 2][:, 0:256]
                            for kc in range(NKC):
                                mm(pbx, merged[:, kc, t_ * 128:(t_ + 1) * 128], wt[:, kc, :], kc == 0, kc == NKC - 1, ["merged", wn], ["pb%d" % (t_ % 2)])
                            act(res(t_)[:, n * 256:(n + 1) * 256], pbx, AF.Copy, ["pb%d" % (t_ % 2)], resn(t_))
                    for t_ in range(4):
                        K.dma("sp", "xl", xt[:, :], x_d[t0 + t_ * 128:t0 + (t_ + 1) * 128, :], writes=["xt"])
                        tt("dve", res(t_), res(t_), xt[:, :], ALU.add, resn(t_) + ["xt"], resn(t_))
                        rstd_rows(res(t_), resn(t_), 128, 4, 5)
                        K.op("dve", lambda e, t_=t_: e.scalar_tensor_tensor(out=res(t_), in0=res(t_), scalar=st[:, 5:6], in1=fnw[:, :],
                                                                            op0=ALU.mult, op1=ALU.mult), reads=resn(t_) + ["st", "fnw"], writes=resn(t_))
                        K.dma("pool", "out%d" % t_, y_d[t0 + t_ * 128:t0 + (t_ + 1) * 128, :], res(t_), reads=resn(t_), writes=["ydram"])
        except _Stop:
            pass
        for eng in ["pe", "act", "dve", "pool", "sp"]:
            K.final_wait(eng)
    return nc


def _host_layout(inp, mirror=False):
    f = np.float32
    inp = dict(inp)
    W = np.asarray(inp["w_in"][0], f)
    mu_ = np.asarray(inp["mu_shift"][0], f)
    if mirror:
        perm = np.arange(NCOLS)
        perm[3072:3168], perm[3168:3264] = np.arange(3168, 3264), np.arange(3072, 3168)
        perm[3264:3360], perm[3360:3456] = np.arange(3360, 3456), np.arange(3264, 3360)
        W = W[:, perm]
        mu_ = mu_[perm[:3456]]
        for k_ in ["w0", "w_up", "a0", "a_up"]:
            inp[k_] = np.asarray(inp[k_])[:, ::-1]
        inp["conv_w"] = np.asarray(inp["conv_w"])[:, ::-1]
    inp["mu_shift"] = mu_[None]

    def blob(cols):
        Wc = W[:, cols]
        n = Wc.shape[1]
        return np.ascontiguousarray(Wc.reshape(16, 128, n).transpose(1, 0, 2).reshape(128, 16 * n))

    ar = np.arange
    out = {}
    Wl = np.zeros((2048, 512), f)
    for g in range(4):
        Wl[:, g * 128:g * 128 + 96] = W[:, 3072 + g * 96:3072 + (g + 1) * 96]
    out["w_l"] = np.ascontiguousarray(Wl.reshape(16, 128, 512).transpose(1, 0, 2).reshape(128, 16 * 512))
    wA, wB, wM = [], [], []
    for p in range(8):
        wA.append(blob(np.concatenate([ar(p * 128, p * 128 + 128), 1024 + ar(p * 128, p * 128 + 128)])))
        wA.append(blob(np.concatenate([2048 + ar(p * 128, p * 128 + 128), 3456 + ar(p * 128, p * 128 + 128)])))
        wB.append(blob(np.concatenate([4480 + ar(p * 128, p * 128 + 128), 5504 + ar(p * 128, p * 128 + 128)])))
        wB.append(blob(np.concatenate([6528 + ar(p * 128, p * 128 + 128), 7552 + ar(p * 128, p * 128 + 128)])))
        wM.append(blob(8576 + ar(p * 256, p * 256 + 256)))
        wM.append(blob(10624 + ar(p * 256, p * 256 + 256)))
    out["w_A"] = np.stack(wA)
    out["w_B"] = np.stack(wB)
    out["w_M"] = np.stack(wM)
    wa = np.asarray(inp["w_a_out"][0], f)
    wb = np.asarray(inp["w_b_out"][0], f)
    wab = []
    for j in range(8):
        a = wa[:, j * 256:(j + 1) * 256].reshape(8, 128, 256).transpose(1, 0, 2)
        b_ = wb[:, j * 256:(j + 1) * 256].reshape(8, 128, 256).transpose(1, 0, 2)
        wab.append(np.stack([a, b_], axis=1).reshape(128, 2 * 8 * 256))
    out["w_ab"] = np.ascontiguousarray(np.stack(wab))
    wo = np.asarray(inp["w_o"][0], f)
    out["w_o"] = np.ascontiguousarray(np.stack([wo[:, n * 256:(n + 1) * 256].reshape(16, 128, 256).transpose(1, 0, 2).reshape(128, 16 * 256)
                                                for n in range(8)]))
    wup_ = np.zeros((128, 4096), f)
    wup_[0:96, :] = np.concatenate([inp["w_up"][0, 0], inp["w_up"][0, 1], inp["a_up"][0, 0], inp["a_up"][0, 1]], axis=1)
    out["w_up"] = wup_
    par = np.zeros((128, NPAR), f)

    def put(name, vec, ncol):
        par[:, PC[name]:PC[name] + ncol] = np.asarray(vec, f).reshape(ncol, 128).T

    put("w0f", inp["w0"][0, 0], 8); put("w0b", inp["w0"][0, 1], 8)
    put("a0f", inp["a0"][0, 0], 8); put("a0b", inp["a0"][0, 1], 8)
    put("kk", inp["k_k"][0], 8); put("ka", inp["k_a"][0], 8); put("rk", np.asarray(inp["r_k"][0]).reshape(1024), 8)
    put("lnw", inp["ln_w"][0], 8); put("lnb", inp["ln_b"][0], 8)
    mu = np.asarray(inp["mu_shift"][0], f)
    put("mur", mu[0:1024], 8); put("muk", mu[1024:2048], 8); put("muv", mu[2048:3072], 8)
    par[0:96, PC["mul"]:PC["mul"] + 4] = mu[3072:3456].reshape(4, 96).T
    put("gba", inp["gate_bias"][0, 0], 16); put("gbb", inp["gate_bias"][0, 1], 16)
    for j in range(3):
        put("cw%d" % j, inp["conv_w"][0, j], 8)
    put("nw", inp["norm_w"][0], 16)
    out["params"] = par
    out["fnw"] = np.ascontiguousarray(np.broadcast_to(np.asarray(inp["final_norm_w"], f)[None, :], (128, D)))
    bf = ml_dtypes.bfloat16
    i = np.arange(128)
    su = (i[:, None] < i[None, :]).astype(f)
    sl = su.T.copy()
    eye = np.eye(128, dtype=f)
    iu = su + eye
    il = sl + eye
    cb = np.zeros((128, 128 + 2 * 640), f)
    cb[:, 0:128] = eye
    cb[:, 128:128 + 640] = np.concatenate([-su, iu, su, iu, -sl], axis=1)
    cb[:, 768:768 + 640] = np.concatenate([-sl, il, sl, il, -su], axis=1)
    out["cbf"] = cb.astype(bf)
    cfa = np.zeros((128, 640), f)
    blk = np.zeros((128, 128), f)
    blk[0:64, 0:64] = 1.0
    blk[64:128, 64:128] = 1.0
    cfa[:, 0:128] = blk
    cfa[:, 128:256] = blk / 64.0
    cfa[0:64, 256:320] = np.eye(64, dtype=f)
    cfa[0:64, 384 + 64:384 + 128] = np.eye(64, dtype=f)
    cfa[:, 512:640] = 1.0
    out["cf32"] = cfa
    return out


_NC_CACHE = {}


def _flags(seq_len, NT):
    NB = NT // BT
    keepL = np.zeros(NB, np.float32)
    for b in range(1, NB):
        keepL[b] = 1.0 if (b * BT) % seq_len != 0 else 0.0
    keepR = np.zeros(NB, np.float32)
    keepR[:-1] = keepL[1:]
    fl2 = np.stack([keepL, keepR])
    flp = np.zeros((128, 2 * NB), np.float32)
    flp[:, 0::2] = keepL[None, :]
    flp[:, 1::2] = keepR[None, :]
    return flp, fl2


def run_streams(streams, seq_lens, inp, NT, NBO=None, mirrors=None):
    if (NT, NBO) not in _NC_CACHE:
        _NC_CACHE[(NT, NBO)] = build_nc(NT, NBO)
    nc = _NC_CACHE[(NT, NBO)]
    mirrors = mirrors or [False] * 8
    lays = {False: _host_layout(inp, False)}
    if any(mirrors):
        lays[True] = _host_layout(inp, True)
    in_maps = []
    for c in range(8):
        flp, fl2 = _flags(seq_lens[c], NT)
        m = dict(lays[bool(mirrors[c])])
        m["x"] = np.ascontiguousarray(streams[c], dtype=np.float32)
        m["flagsP"] = flp
        m["flags2"] = fl2
        in_maps.append(m)
    res = run_bass_kernel_spmd(nc, in_maps, core_ids=list(range(8)))
    return [res.results[c]["y"] for c in range(8)]


def kernel(**inp):
    xp = np.asarray(inp["x_prompt"], np.float32)
    xsm = np.asarray(inp["x_sample"], np.float32)
    NT, NBO = 16384, 16
    H = NT // 2
    zeros = np.zeros((H, D), np.float32)
    streams = [xp[0], xp[0][::-1], xp[1], xp[1][::-1]]
    for c in range(4):
        streams.append(np.concatenate([xsm[4 * c:4 * c + 4].reshape(H, D), zeros], axis=0))
    seq_lens = [16384] * 4 + [2048] * 4
    mirrors = [False, True, False, True, False, False, False, False]
    ys = run_streams(streams, seq_lens, inp, NT, NBO, mirrors)
    y_prompt = np.stack([np.concatenate([ys[0], ys[1][::-1]], axis=0), np.concatenate([ys[2], ys[3][::-1]], axis=0)]).astype(np.float32)
    y_sample = np.concatenate([ys[4 + c].reshape(4, 2048, D) for c in range(4)], axis=0).astype(np.float32)
    return (y_prompt, y_sample)
```

```python
import numpy as np
import ml_dtypes
from contextlib import ExitStack
import concourse.bass as bass
import concourse.mybir as mybir
from concourse.bass_utils import run_bass_kernel_spmd

F32 = mybir.dt.float32
BF16 = mybir.dt.bfloat16
AF = mybir.ActivationFunctionType
ALU = mybir.AluOpType

D = 2048
NKC = 16
NCOLS = 12672
BT = 512
NCH = BT // 128
C0 = float(np.exp(-0.5))
RMS_EPS = 1e-6
GN_EPS = 64e-5

PC = {}
_o = 0
for _n, _w in [("w0f", 8), ("w0b", 8), ("a0f", 8), ("a0b", 8), ("kk", 8), ("ka", 8), ("rk", 8), ("lnw", 8), ("lnb", 8),
               ("mur", 8), ("muk", 8), ("muv", 8), ("mul", 4), ("gba", 16), ("gbb", 16), ("cw0", 8), ("cw1", 8), ("cw2", 8),
               ("nw", 16)]:
    PC[_n] = _o
    _o += _w
NPAR = _o


class Ctx:
    def __init__(self, nc, es):
        self.nc = nc
        self.es = es
        self.E = {"pe": nc.tensor, "act": nc.scalar, "dve": nc.vector, "pool": nc.gpsimd, "sp": nc.sync}
        self.sem = {k: es.enter_context(nc.semaphore("sem_" + k)) for k in self.E}
        self.cnt = {k: 0 for k in self.E}
        self.dsem = {}
        self.dcnt = {}
        self.waited = {k: {} for k in self.E}
        self.res = {}

    def _semh(self, key):
        return self.sem[key] if key in self.sem else self.dsem[key]

    def _wait(self, eng, key, val):
        if self.waited[eng].get(key, 0) >= val:
            return
        self.waited[eng][key] = val
        self.E[eng].wait_ge(self._semh(key), val)

    def _deps(self, eng, reads, writes):
        toks = {}

        def add(t):
            if t is not None and toks.get(t[0], 0) < t[1]:
                toks[t[0]] = t[1]

        for r in reads:
            st = self.res.get(r)
            if st:
                add(st[0])
        for w in writes:
            st = self.res.get(w)
            if st:
                add(st[0])
                for k, v in st[1].items():
                    add((k, v))
        for k, v in toks.items():
            if k == eng and eng == "pe":
                continue
            self._wait(eng, k, v)

    def _commit(self, tok, reads, writes):
        for r in reads:
            st = self.res.setdefault(r, [None, {}])
            if st[1].get(tok[0], 0) < tok[1]:
                st[1][tok[0]] = tok[1]
        for w in writes:
            self.res[w] = [tok, {}]

    def op(self, eng, fn, reads=(), writes=()):
        self._deps(eng, reads, writes)
        inst = fn(self.E[eng])
        self.cnt[eng] += 1
        inst.then_inc(self.sem[eng], 1)
        self._commit((eng, self.cnt[eng]), reads, writes)

    def dma(self, q, semkey, out, in_, reads=(), writes=()):
        if semkey not in self.dsem:
            self.dsem[semkey] = self.es.enter_context(self.nc.semaphore("dsem_" + semkey))
            self.dcnt[semkey] = 0
        self._deps(q, reads, writes)
        inst = self.E[q].dma_start(out=out, in_=in_)
        self.dcnt[semkey] += 16
        inst.then_inc(self.dsem[semkey], 16)
        self._commit((semkey, self.dcnt[semkey]), reads, writes)

    def final_wait(self, eng="sp"):
        for k in list(self.sem) + list(self.dsem):
            v = self.cnt[k] if k in self.cnt else self.dcnt[k]
            if v > 0:
                self._wait(eng, k, v)


class _Stop(Exception):
    pass


def build_nc(NT, NBO=None, STOP=0):
    NB = NT // BT
    if NBO is None:
        NBO = NB
    NTO = NBO * BT

    def ck(n):
        if STOP == n:
            raise _Stop()
    nc = bass.Bass("TRN2", target_bir_lowering=False)
    dt = nc.dram_tensor
    x_d = dt("x", [NT, D], F32, kind="ExternalInput").ap()
    wl_d = dt("w_l", [128, NKC * 512], F32, kind="ExternalInput").ap()
    wA_d = dt("w_A", [16, 128, NKC * 256], F32, kind="ExternalInput").ap()
    wB_d = dt("w_B", [16, 128, NKC * 256], F32, kind="ExternalInput").ap()
    wM_d = dt("w_M", [16, 128, NKC * 256], F32, kind="ExternalInput").ap()
    wab_d = dt("w_ab", [8, 128, 2 * 8 * 256], F32, kind="ExternalInput").ap()
    wo_d = dt("w_o", [8, 128, NKC * 256], F32, kind="ExternalInput").ap()
    wup_d = dt("w_up", [128, 4 * 1024], F32, kind="ExternalInput").ap()
    par_d = dt("params", [128, NPAR], F32, kind="ExternalInput").ap()
    fnw_d = dt("fnw", [128, D], F32, kind="ExternalInput").ap()
    flp_d = dt("flagsP", [128, 2 * NB], F32, kind="ExternalInput").ap()
    fl2_d = dt("flags2", [2, NB], F32, kind="ExternalInput").ap()
    cbf_d = dt("cbf", [128, 128 + 2 * 640], BF16, kind="ExternalInput").ap()
    cf_d = dt("cf32", [128, 5 * 128], F32, kind="ExternalInput").ap()
    y_d = dt("y", [NTO, D], F32, kind="ExternalOutput").ap()
    swl = dt("s_wl", [128, NKC * 512], BF16, kind="Internal").ap()
    swA = dt("s_wA", [16, 128, NKC * 256], BF16, kind="Internal").ap()
    swB = dt("s_wB", [16, 128, NKC * 256], BF16, kind="Internal").ap()
    swM = dt("s_wM", [16, 128, NKC * 256], BF16, kind="Internal").ap()
    swab = dt("s_wab", [8, 128, 2 * 8 * 256], BF16, kind="Internal").ap()
    swo = dt("s_wo", [8, 128, NKC * 256], BF16, kind="Internal").ap()
    yscr = dt("s_yf", [8, 128, NTO], F32, kind="Internal").ap()

    es = ExitStack()
    with es:
        K = Ctx(nc, es)

        def sb(name, shape, dtype):
            return es.enter_context(nc.sbuf_tensor("sb_" + name, shape, dtype))

        def ps(name, shape, dtype):
            return es.enter_context(nc.psum_tensor("ps_" + name, shape, dtype))

        cbf = sb("cbf", [128, 128 + 2 * 640], BF16)
        cf = sb("cf", [128, 5 * 128], F32)
        par = sb("par", [128, NPAR], F32)
        par1 = sb("par1", [128, NPAR], F32)
        par2 = sb("par2", [128, NPAR], F32)
        fnw = sb("fnw", [128, D], F32)
        flp = sb("flp", [128, 2 * NB], F32)
        fl2 = sb("fl2", [2, NB], F32)
        wup = sb("wup", [128, 4 * 1024], BF16)
        K.dma("sp", "c0", cbf[:, :], cbf_d[:, :], writes=["cbf"])
        K.dma("sp", "c0", cf[:, :], cf_d[:, :], writes=["cf"])
        K.dma("sp", "c0", par[:, :], par_d[:, :], writes=["par"])
        K.dma("sp", "c0", fnw[:, :], fnw_d[:, :], writes=["fnw"])
        K.dma("sp", "c0", flp[:, :], flp_d[:, :], writes=["flp"])
        K.dma("sp", "c0", fl2[:, :], fl2_d[:, :], writes=["fl2"])
        for _nm in ["cbf", "cf", "par", "fnw", "flp", "fl2"]:
            K.res[_nm][0] = ("c0", K.dcnt["c0"])
        ident = cbf[:, 0:128]

        def masks(d):
            o = 128 + d * 640
            return cbf[:, o:o + 256], cbf[:, o + 256:o + 512], cbf[:, o + 512:o + 640]

        blockones = cf[:, 0:128]
        blockavg = cf[:, 128:256]
        sel0 = cf[0:64, 256:384]
        sel1 = cf[0:64, 384:512]
        ones128 = cf[:, 512:640]
        K.op("dve", lambda e: e.tensor_scalar(out=par1[:, :], in0=par[:, :], scalar1=-1.0, scalar2=1.0,
                                              op0=ALU.mult, op1=ALU.add), reads=["par"], writes=["par1"])
        K.op("dve", lambda e: e.tensor_scalar(out=par2[:, :], in0=par[:, :], scalar1=0.5, scalar2=None,
                                              op0=ALU.mult), reads=["par"], writes=["par2"])

        def pcol(t, name, i, np_=128):
            return t[0:np_, PC[name] + i:PC[name] + i + 1]

        with nc.sbuf_tensor("pst_f", [128, 2, 4096], F32) as stg_f, nc.sbuf_tensor("pst_b", [128, 2, 4096], BF16) as stg_b:
            jobs = [(wl_d[:, :], swl[:, :], NKC * 512)]
            for src, dst, n, F in [(wA_d, swA, 16, NKC * 256), (wB_d, swB, 16, NKC * 256), (wM_d, swM, 16, NKC * 256),
                                   (wab_d, swab, 8, 4096), (wo_d, swo, 8, NKC * 256)]:
                for i in range(n):
                    jobs.append((src[i, :, :], dst[i, :, :], F))
            ci = 0
            for (src, dst, F) in jobs:
                for c0 in range(0, F, 4096):
                    w = min(4096, F - c0)
                    s = ci % 2
                    K.dma("sp", "pcl%d" % s, stg_f[:, s, 0:w], src[:, c0:c0 + w], writes=["stg_f%d" % s])
                    eng = ["dve", "act", "pool"][ci % 3]
                    if eng == "act":
                        K.op("act", lambda e, s=s, w=w: e.activation(out=stg_b[:, s, 0:w], in_=stg_f[:, s, 0:w], func=AF.Copy),
                             reads=["stg_f%d" % s], writes=["stg_b%d" % s])
                    else:
                        K.op(eng, lambda e, s=s, w=w: e.tensor_copy(out=stg_b[:, s, 0:w], in_=stg_f[:, s, 0:w]),
                             reads=["stg_f%d" % s], writes=["stg_b%d" % s])
                    K.dma("pool", "pcs%d" % s, dst[:, c0:c0 + w], stg_b[:, s, 0:w], reads=["stg_b%d" % s], writes=["wscr"])
                    ci += 1
            K.dma("sp", "pcl0", stg_f[:, 0, :], wup_d[:, :], writes=["stg_f0"])
            K.op("dve", lambda e: e.tensor_copy(out=wup[:, :], in_=stg_f[:, 0, :]), reads=["stg_f0"], writes=["wup"])
            for eng in ["pe", "act", "dve", "pool", "sp"]:
                K.final_wait(eng)

        NS = 18
        pool = sb("pool", [128, NS, BT], F32)
        xt = sb("xt", [128, D], F32)
        xs = sb("xs", [128, D], BF16)
        st = sb("stats", [128, 8], F32)
        hT = sb("hT", [128, NKC, BT], BF16)
        hTh = sb("hTh", [128, NKC, 2], BF16)
        wbf = [sb("wbf%d" % i, [128, NKC * 512], BF16) for i in range(2)]
        wabt = sb("wab", [128, 2, 8, 256], BF16)
        twd = sb("twd", [128, BT], BF16)
        adb = sb("adb", [128, BT], BF16)
        zf = sb("zf", [128, BT + 2], F32)
        KR = sb("KR", [128, NCH, 2, 128], BF16)
        R2 = sb("R2", [128, NCH, 2, 128], BF16)
        BTt = sb("BTt", [128, BT], BF16)
        KTt = sb("KTt", [128, BT], BF16)
        vbf = sb("vbf", [128, BT], BF16)
        Vt = sb("Vt", [128, NCH, 128], BF16)
        BKt = sb("BKt", [128, NCH, 2, 128], BF16)
        MAt = sb("MAt", [128, 2 * NCH, 256], BF16)
        AKt = sb("AKt", [128, 2 * NCH, 256], BF16)
        XTt = sb("XTt", [128, 2 * NCH, 128], BF16)
        NM = sb("NM", [128, NCH, 2, 256], BF16)
        XTw = sb("XTw", [128, NCH, 2, 128], BF16)
        Xn = sb("Xn", [128, 128], BF16)
        Ub = sb("Ub", [128, 128], BF16)
        Hf = sb("Hf", [128, 8, 64], F32)
        Hb = sb("Hb", [128, 8, 64], BF16)
        H0P = sb("H0P", [128, 64], F32)
        ysb = sb("ysb", [64, 2, BT], F32)
        yg = sb("yg", [128, 8, BT], BF16)
        ybg = sb("ybg", [128, 8, BT], BF16)
        merged = sb("merged", [128, NKC, BT], BF16)
        uf = zf
        hal = sb("hal", [128, 4], F32)

        pbank = [ps("pb%d" % i, [128, 512], F32) for i in range(5)]
        ptr = ps("ptr", [128, 1024], BF16)
        yps = ps("yps", [64, NCH, 2, 128], F32)

        def S(i):
            return pool[:, i, :]

        def SN(i):
            return "pool%d" % i

        K.op("pool", lambda e: e.memset(R2[:, :, :, :].rearrange("p a b c -> p (a b c)"), 0.0), writes=["R2"])

        wq = []
        wstate = {"issued": 0, "used": 0}

        def w_issue():
            i = wstate["issued"]
            s = i % 2
            src, ncol = wq[i]
            K.dma("sp", "wl%d" % s, wbf[s][:, 0:NKC * ncol], src, reads=["wscr"], writes=["wbf%d" % s])
            wstate["issued"] += 1

        def w_next():
            i = wstate["used"]
            while wstate["issued"] <= min(i + 1, len(wq) - 1):
                w_issue()
            wstate["used"] += 1
            ncol = wq[i][1]
            return wbf[i % 2][:, 0:NKC * ncol].rearrange("p (k n) -> p k n", n=ncol), "wbf%d" % (i % 2)

        for pas in range(2):
            for bi in (list(range(NBO)) if pas == 0 else list(range(NB - 1, -1, -1))):
                wq.append((swl[:, :], 512))
                for p in range(16):
                    wq.append((swA[p, :, :], 256))
                if pas == 1 and bi < NBO:
                    for q in range(16):
                        wq.append((swB[q, :, :], 256))
                    for j in range(16):
                        wq.append((swM[j, :, :], 256))
                    for n in range(8):
                        wq.append((swo[n, :, :], 256))

        hps_col = [0]

        def inproj(wt, wname, col0, M, pmain, pmain_name, halo):
            hap = None
            if halo:
                c = hps_col[0] % 64
                hps_col[0] += 1
                hap = pbank[2][0:M, 4 * c:4 * c + 2]
            for kc in range(NKC):
                K.op("pe", lambda e, kc=kc: e.matmul(pmain, wt[:, kc, col0:col0 + M], hT[:, kc, :], start=(kc == 0), stop=(kc == NKC - 1)),
                     reads=[wname, "hT"], writes=[pmain_name])
            ck(22)
            if halo:
                for kc in range(NKC):
                    K.op("pe", lambda e, kc=kc: e.matmul(hap, wt[:, kc, col0:col0 + M], hTh[:, kc, :], start=(kc == 0), stop=(kc == NKC - 1)),
                         reads=[wname, "hTh"], writes=["pb2"])
            ck(23)
            return hap

        def shiftmix(M, pmain, pmain_name, hap, c1, c2, out_i):
            K.op("act", lambda e: e.activation(out=zf[0:M, 1:BT + 1], in_=pmain, func=AF.Copy), reads=[pmain_name], writes=["zf"])
            K.op("act", lambda e: e.activation(out=zf[0:M, 0:1], in_=hap[:, 0:1], func=AF.Copy), reads=["pb2"], writes=["zf"])
            K.op("act", lambda e: e.activation(out=zf[0:M, BT + 1:BT + 2], in_=hap[:, 1:2], func=AF.Copy), reads=["pb2"], writes=["zf"])
            ck(24)
            K.op("dve", lambda e: e.tensor_scalar(out=S(0)[0:M, :], in0=zf[0:M, 1:BT + 1], scalar1=c1, scalar2=None, op0=ALU.mult),
                 reads=["zf", "par1"], writes=[SN(0)])
            ck(25)
            K.op("dve", lambda e: e.tensor_tensor(out=S(1)[0:M, :], in0=zf[0:M, 0:BT], in1=zf[0:M, 2:BT + 2], op=ALU.add),
                 reads=["zf"], writes=[SN(1)])
            K.op("dve", lambda e: e.scalar_tensor_tensor(out=S(out_i)[0:M, :], in0=S(1)[0:M, :], scalar=c2, in1=S(0)[0:M, :],
                                                         op0=ALU.mult, op1=ALU.add),
                 reads=[SN(0), SN(1), "par2"], writes=[SN(out_i)])

        def rstd_rows(src, srcnames, npart, c_ss, c_out, flagcol=None):
            K.op("pool", lambda e: e.memset(st[0:npart, c_ss:c_ss + 1], 0.0), writes=["st"])
            K.op("act", lambda e: e.activation(out=xs[0:npart, :], in_=src, func=AF.Square, accum_out=st[0:npart, c_ss:c_ss + 1]),
                 reads=list(srcnames) + ["st"], writes=["xs", "st"])
            K.op("dve", lambda e: e.tensor_scalar(out=st[0:npart, c_out:c_out + 1], in0=st[0:npart, c_ss:c_ss + 1],
                                                  scalar1=1.0 / D, scalar2=RMS_EPS, op0=ALU.mult, op1=ALU.add),
                 reads=["st"], writes=["st"])
            K.op("act", lambda e: e.activation(out=st[0:npart, c_out:c_out + 1], in_=st[0:npart, c_out:c_out + 1], func=AF.Sqrt),
                 reads=["st"], writes=["st"])
            K.op("dve", lambda e: e.reciprocal(out=st[0:npart, c_out:c_out + 1], in_=st[0:npart, c_out:c_out + 1]),
                 reads=["st"], writes=["st"])
            if flagcol is not None:
                K.op("dve", lambda e: e.tensor_scalar(out=st[0:npart, c_out:c_out + 1], in0=st[0:npart, c_out:c_out + 1],
                                                      scalar1=flagcol, scalar2=None, op0=ALU.mult),
                     reads=["st", "fl2"], writes=["st"])

        def build_hT(b):
            t0 = b * BT
            for i in range(BT // 128):
                K.dma("sp", "xl", xt[:, :], x_d[t0 + i * 128:t0 + (i + 1) * 128, :], writes=["xt"])
                rstd_rows(xt[:, :], ["xt"], 128, 0, 1)
                K.op("dve", lambda e: e.tensor_scalar(out=xs[:, :], in0=xt[:, :], scalar1=st[:, 1:2], scalar2=None, op0=ALU.mult),
                     reads=["xt", "st"], writes=["xs"])
                for half in range(2):
                    for k8 in range(8):
                        kc = half * 8 + k8
                        K.op("pe", lambda e, kc=kc, k8=k8: e.transpose(ptr[:, k8 * 128:(k8 + 1) * 128], xs[:, kc * 128:(kc + 1) * 128], ident),
                             reads=["xs", "cbf"], writes=["ptr"])
                    for k8 in range(8):
                        kc = half * 8 + k8
                        K.op("dve", lambda e, kc=kc, k8=k8, i=i: e.tensor_scalar(out=hT[:, kc, i * 128:(i + 1) * 128],
                                                                                 in0=ptr[:, k8 * 128:(k8 + 1) * 128],
                                                                                 scalar1=pcol(par, "nw", kc), scalar2=None, op0=ALU.mult),
                             reads=["ptr", "par"], writes=["hT"])
            tl = max(t0 - 1, 0)
            tr = min(t0 + BT, NT - 1)
            K.dma("sp", "xhl", xt[0:1, :], x_d[tl:tl + 1, :], writes=["xt"])
            K.dma("sp", "xhl", xt[1:2, :], x_d[tr:tr + 1, :], writes=["xt"])
            rstd_rows(xt[0:2, :], ["xt"], 2, 2, 3, flagcol=fl2[0:2, b:b + 1])
            K.op("dve", lambda e: e.tensor_scalar(out=xs[0:2, :], in0=xt[0:2, :], scalar1=st[0:2, 3:4], scalar2=None, op0=ALU.mult),
                 reads=["xt", "st"], writes=["xs"])
            for kc in range(NKC):
                K.op("pe", lambda e, kc=kc: e.transpose(ptr[:, 2 * kc:2 * kc + 2], xs[0:2, kc * 128:(kc + 1) * 128], ident[0:2, 0:2]),
                     reads=["xs", "cbf"], writes=["ptr"])
            for kc in range(NKC):
                K.op("dve", lambda e, kc=kc: e.tensor_scalar(out=hTh[:, kc, :], in0=ptr[:, 2 * kc:2 * kc + 2], scalar1=pcol(par, "nw", kc),
                                                             scalar2=None, op0=ALU.mult),
                     reads=["ptr", "par"], writes=["hTh"])

        def c3(ap):
            return ap.rearrange("p (c t) -> p c t", t=128)

        def f2(ap):
            return ap.rearrange("p a t -> p (a t)")

        def tt(eng, out, in0, in1, op, reads, writes):
            K.op(eng, lambda e: e.tensor_tensor(out=out, in0=in0, in1=in1, op=op), reads=reads, writes=writes)

        def act(out, in_, func, reads, writes, **kw):
            K.op("act", lambda e: e.activation(out=out, in_=in_, func=func, **kw), reads=reads, writes=writes)

        def mm(out, lhsT, rhs, start, stop, reads, writes):
            K.op("pe", lambda e: e.matmul(out, lhsT, rhs, start=start, stop=stop), reads=reads, writes=writes)

        pb0, pb1, pb2, pb3, pb4 = pbank[0][:, :], pbank[1][:, :], pbank[2], pbank[3], pbank[4]
        Xps = pb2[:, 256:384]
        Ups = pb2[:, 384:512]
        ps1 = pb3[:, 0:256]
        ps2 = pb2[:, 0:256]
        ps3 = pb4[:, 0:128]
        sqN = pb4[:, 128:256]
        sqM = pbank[0][:, 0:128]
        xu = pbank[1][:, 0:128]

        try:
            for pas in range(2):
                d = pas
                mMA, mAK, mN = masks(d)
                K.op("pool", lambda e: e.memset(Hf[:, :, :], 0.0), writes=["Hf"])
                K.op("pool", lambda e: e.memset(Hb[:, :, :], 0.0), writes=["Hb"])
                blocks = list(range(NBO)) if pas == 0 else list(range(NB - 1, -1, -1))
                for b in blocks:
                    so = (pas == 1 and b >= NBO)
                    t0 = b * BT
                    ck(1)
                    build_hT(b)
                    fcol = flp[:, 2 * b + d:2 * b + d + 1]
                    K.op("dve", lambda e: e.tensor_scalar(out=Hf[:, :, :], in0=Hf[:, :, :], scalar1=fcol, scalar2=None, op0=ALU.mult),
                         reads=["Hf", "flp"], writes=["Hf"])
                    K.op("pool", lambda e: e.tensor_copy(out=Hb[:, :, :], in_=Hf[:, :, :]), reads=["Hf"], writes=["Hb"])
                    ck(21)
                    ck(2)
                    wt, wn = w_next()
                    for li, (cc, dst, fn) in enumerate([(d * 128, twd, AF.Tanh), (256 + d * 128, adb, AF.Copy)]):
                        pm = pbank[li][:, :]
                        hap = inproj(wt, wn, cc, 128, pm, "pb%d" % li, True)
                        mi = (0 if li == 0 else 2) + d
                        shiftmix(128, pm, "pb%d" % li, hap, pcol(par1, "mul", mi), pcol(par2, "mul", mi), 2)
                        if li == 0:
                            act(S(2), S(2), AF.Sigmoid, [SN(2)], [SN(2)], scale=2.0)
                            K.op("dve", lambda e: e.tensor_scalar(out=twd[:, :], in0=S(2), scalar1=2.0, scalar2=-1.0,
                                                                  op0=ALU.mult, op1=ALU.add), reads=[SN(2)], writes=["lora0"])
                        else:
                            K.op("dve", lambda e: e.tensor_copy(out=adb[:, :], in_=S(2)), reads=[SN(2)], writes=["lora1"])
                    for p in range(8):
                        ck(3)
                        wt, wn = w_next()
                        if not so:
                            hap = inproj(wt, wn, 0, 128, pb0, "pb0", True)
                            shiftmix(128, pb0, "pb0", hap, pcol(par1, "mur", p), pcol(par2, "mur", p), 3)
                        hap = inproj(wt, wn, 128, 128, pb1, "pb1", True)
                        shiftmix(128, pb1, "pb1", hap, pcol(par1, "muk", p), pcol(par2, "muk", p), 4)
                        wt, wn = w_next()
                        hap = inproj(wt, wn, 0, 128, pb0, "pb0", True)
                        shiftmix(128, pb0, "pb0", hap, pcol(par1, "muv", p), pcol(par2, "muv", p), 5)
                        if pas == 1 and not so:
                            inproj(wt, wn, 128, 128, pb1, "pb1", False)
                            act(S(17), pb1, AF.Sigmoid, ["pb1"], [SN(17)])
                            tt("dve", S(17), S(17), pb1, ALU.mult, [SN(17), "pb1"], [SN(17)])
                        ck(4)
                        mm(pb0, wup[:, d * 1024 + p * 128:d * 1024 + (p + 1) * 128], twd[:, :], True, True, ["wup", "lora0"], ["pb0"])
                        act(S(6), pb0, AF.Sigmoid, ["pb0", "par"], [SN(6)], bias=pcol(par, "w0b" if d else "w0f", p))
                        mm(pb1, wup[:, (2 + d) * 1024 + p * 128:(2 + d) * 1024 + (p + 1) * 128], adb[:, :], True, True, ["wup", "lora1"], ["pb1"])
                        act(S(7), pb1, AF.Sigmoid, ["pb1", "par"], [SN(7)], bias=pcol(par, "a0b" if d else "a0f", p))
                        K.op("dve", lambda e: e.tensor_scalar(out=S(8), in0=S(4), scalar1=pcol(par, "kk", p), scalar2=None, op0=ALU.mult),
                             reads=[SN(4), "par"], writes=[SN(8)])
                        tt("dve", S(0), S(8), S(8), ALU.mult, [SN(8)], [SN(0)])
                        mm(pb0, blockones, S(0), True, True, ["cf", SN(0)], ["pb0"])
                        act(S(9), pb0, AF.Sqrt, ["pb0"], [SN(9)])
                        K.op("dve", lambda e: e.tensor_scalar(out=S(9), in0=S(9), scalar1=1e-12, scalar2=None, op0=ALU.max),
                             reads=[SN(9)], writes=[SN(9)])
                        K.op("dve", lambda e: e.reciprocal(out=S(9), in_=S(9)), reads=[SN(9)], writes=[SN(9)])
                        tt("dve", S(8), S(8), S(9), ALU.mult, [SN(8), SN(9)], [SN(8)])
                        K.op("dve", lambda e: e.tensor_scalar(out=S(10), in0=S(7), scalar1=-1.0, scalar2=pcol(par, "ka", p),
                                                              op0=ALU.add, op1=ALU.mult), reads=[SN(7), "par"], writes=[SN(10)])
                        K.op("dve", lambda e: e.scalar_tensor_tensor(out=S(10), in0=S(10), scalar=1.0, in1=S(4), op0=ALU.add, op1=ALU.mult),
                             reads=[SN(10), SN(4)], writes=[SN(10)])
                        tt("dve", S(7), S(8), S(7), ALU.mult, [SN(8), SN(7)], [SN(7)])
                        for c in range(NCH):
                            cs = slice(c * 128, (c + 1) * 128)
                            K.op("dve", lambda e, cs=cs: e.tensor_tensor_scan(out=S(11)[:, cs], data0=ones128, data1=S(6)[:, cs], initial=0.0,
                                                                              op0=ALU.mult, op1=ALU.add), reads=["cf", SN(6)], writes=[SN(11)])
                        if d == 1:
                            for c in range(NCH):
                                cs = slice(c * 128, (c + 1) * 128)
                                K.op("dve", lambda e, cs=cs, c=c: e.tensor_scalar(out=S(12)[:, cs], in0=S(11)[:, cs],
                                                                                  scalar1=S(11)[:, c * 128 + 127:c * 128 + 128], scalar2=-1.0,
                                                                                  op0=ALU.subtract, op1=ALU.mult), reads=[SN(11)], writes=[SN(12)])
                            tt("dve", S(11), S(12), S(6), ALU.add, [SN(12), SN(6)], [SN(11)])
                        tt("dve", S(12), S(11), S(6), ALU.subtract, [SN(11), SN(6)], [SN(12)])
                        act(S(13), S(11), AF.Exp, [SN(11)], [SN(13)], scale=-C0)
                        act(S(14), S(11), AF.Exp, [SN(11)], [SN(14)], scale=C0)
                        act(S(15), S(12), AF.Exp, [SN(12)], [SN(15)], scale=-C0)
                        tt("dve", KR[:, :, 0, :], c3(S(8)), c3(S(15)), ALU.mult, [SN(8), SN(15)], ["KR"])
                        if not so:
                            tt("dve", KR[:, :, 1, :], c3(S(3)), c3(S(13)), ALU.mult, [SN(3), SN(13)], ["KR"])
                            tt("dve", R2[0:64, :, 0, :], c3(S(3)[0:64, :]), c3(S(13)[0:64, :]), ALU.mult, [SN(3), SN(13)], ["R2"])
                            tt("dve", R2[64:128, :, 1, :], c3(S(3)[64:128, :]), c3(S(13)[64:128, :]), ALU.mult, [SN(3), SN(13)], ["R2"])
                        tt("dve", BTt[:, :], S(7), S(14), ALU.mult, [SN(7), SN(14)], ["BTt"])
                        tt("dve", KTt[:, :], S(10), S(14), ALU.mult, [SN(10), SN(14)], ["KTt"])
                        if not so:
                            K.op("dve", lambda e: e.scalar_tensor_tensor(out=S(0), in0=S(3), scalar=pcol(par, "rk", p), in1=S(10),
                                                                         op0=ALU.mult, op1=ALU.mult), reads=[SN(3), SN(10), "par"], writes=[SN(0)])
                            mm(pb1, blockones, S(0), True, True, ["cf", SN(0)], ["pb1"])
                            tt("dve", S(16), pb1, S(5), ALU.mult, ["pb1", SN(5)], [SN(16)])
                        act(vbf[:, :], S(5), AF.Copy, [SN(5)], ["vbf"])
                        for c in range(NCH):
                            K.op("pe", lambda e, c=c: e.transpose(ptr[:, c * 128:(c + 1) * 128], vbf[:, c * 128:(c + 1) * 128], ident),
                                 reads=["vbf", "cbf"], writes=["ptr"])
                        act(f2(Vt[:, :, :]), ptr[:, 0:512], AF.Copy, ["ptr"], ["Vt"])
                        for c in range(NCH):
                            K.op("pe", lambda e, c=c: e.transpose(ptr[:, 512 + c * 128:512 + (c + 1) * 128], BTt[:, c * 128:(c + 1) * 128], ident),
                                 reads=["BTt", "cbf"], writes=["ptr"])
                        K.op("dve", lambda e: e.tensor_copy(out=BKt[:, :, 0, :], in_=c3(ptr[:, 512:1024])), reads=["ptr"], writes=["BKt"])
                        for c in range(NCH):
                            K.op("pe", lambda e, c=c: e.transpose(ptr[:, c * 128:(c + 1) * 128], KTt[:, c * 128:(c + 1) * 128], ident),
                                 reads=["KTt", "cbf"], writes=["ptr"])
                        act(BKt[:, :, 1, :], c3(ptr[:, 0:512]), AF.Copy, ["ptr"], ["BKt"])
                        ck(5)
                        for h in range(2):
                            hp = slice(64 * h, 64 * h + 64)
                            nw_ = 128 if so else 256
                            bks = [0, 1, 3, 4]
                            for c in range(NCH):
                                cs = slice(c * 128, (c + 1) * 128)
                                bk, bn = pbank[bks[c]], "pb%d" % bks[c]
                                KRh = KR[hp, c, 0, :] if so else f2(KR[hp, c, :, :])
                                mm(bk[:, 0:nw_], BTt[hp, cs], KRh, True, True, ["BTt", "KR"], [bn])
                                mm(bk[:, 256:256 + nw_], KTt[hp, cs], KRh, True, True, ["KTt", "KR"], [bn])
                            for c in range(NCH):
                                cs = slice(c * 128, (c + 1) * 128)
                                mm(pb2[:, cs], KR[hp, c, 0, :], BTt[hp, cs], True, True, ["BTt", "KR"], ["pb2"])
                            for c in range(NCH):
                                hc = h * NCH + c
                                cs = slice(c * 128, (c + 1) * 128)
                                bk, bn = pbank[bks[c]], "pb%d" % bks[c]
                                tt("dve", MAt[:, hc, 0:nw_], bk[:, 0:nw_], mMA[:, 0:nw_], ALU.mult, [bn, "cbf"], ["MAt%d" % hc])
                                tt("dve", AKt[:, hc, 0:nw_], bk[:, 256:256 + nw_], mAK[:, 0:nw_], ALU.mult, [bn, "cbf"], ["AKt%d" % hc])
                                tt("dve", NM[:, c, 0, 0:128], pb2[:, cs], mN, ALU.mult, ["pb2", "cbf"], ["NM%d_0" % c])
                                tt("dve", XTw[:, c, 0, :], MAt[:, hc, 0:128], ident, ALU.add, ["MAt%d" % hc, "cbf"], ["XTw%d_0" % c])
                            cur = []
                            for c in range(NCH):
                                hc = h * NCH + c
                                cur.append([NM[:, c, 0, 0:128], "NM%d_0" % c, MAt[:, hc, 0:128], "MAt%d" % hc, XTw[:, c, 0, :], "XTw%d_0" % c])
                            for j in range(1, 7):
                                s = j % 2
                                for c in range(NCH):
                                    bk, bn = pbank[bks[c]], "pb%d" % bks[c]
                                    Ncur, Nname, Mcur, Mname, Xcur, Xname = cur[c]
                                    mm(bk[:, 0:128], Mcur, Ncur, True, True, [Mname, Nname], [bn])
                                    if j < 6:
                                        mm(bk[:, 128:256], Ncur, Mcur, True, True, [Mname, Nname], [bn])
                                for c in range(NCH):
                                    bk, bn = pbank[bks[c]], "pb%d" % bks[c]
                                    wd_ = 256 if j < 6 else 128
                                    act(NM[:, c, s, 0:wd_], bk[:, 0:wd_], AF.Copy, [bn], ["NM%d_%d" % (c, s)])
                                    cur[c][0], cur[c][1] = NM[:, c, s, 0:128], "NM%d_%d" % (c, s)
                                    if j < 6:
                                        cur[c][2], cur[c][3] = NM[:, c, s, 128:256], "NM%d_%d" % (c, s)
                                for c in range(NCH):
                                    bk, bn = pbank[bks[c]], "pb%d" % bks[c]
                                    mm(bk[:, 256:384], cur[c][0], cur[c][4], True, True, [cur[c][1], cur[c][5]], [bn])
                                for c in range(NCH):
                                    hc = h * NCH + c
                                    bk, bn = pbank[bks[c]], "pb%d" % bks[c]
                                    if j < 6:
                                        Xnew, Xnn = XTw[:, c, s, :], "XTw%d_%d" % (c, s)
                                    else:
                                        Xnew, Xnn = XTt[:, hc, :], "XTt%d" % hc
                                    tt("dve", Xnew, bk[:, 256:384], cur[c][4], ALU.add, [bn, cur[c][5]], [Xnn])
                                    cur[c][4], cur[c][5] = Xnew, Xnn
                        ck(6)
                        order = list(range(NCH)) if d == 0 else list(range(NCH - 1, -1, -1))
                        for c in order:
                            pidx = c * 128 + (127 if d == 0 else 0)
                            ptot = S(13)[:, pidx:pidx + 1]
                            K.op("dve", lambda e, ptot=ptot: e.tensor_scalar(out=H0P[:, :], in0=Hf[:, p, :], scalar1=ptot, scalar2=None, op0=ALU.mult),
                                 reads=["Hf", SN(13)], writes=["H0P"])
                            for h in range(2):
                                hp = slice(64 * h, 64 * h + 64)
                                hc = h * NCH + c
                                hs = slice(64 * h, 64 * h + 64)
                                mm(Xps[:, hs], AKt[:, hc, 0:128], Vt[:, c, hs], True, False, ["AKt%d" % hc, "Vt"], ["pb2"])
                                mm(Xps[:, hs], KR[hp, c, 0, :], Hb[hp, p, :], False, True, ["KR", "Hb"], ["pb2"])
                            act(Xn[:, :], Xps, AF.Copy, ["pb2"], ["Xn"], scale=-1.0)
                            for h in range(2):
                                hc = h * NCH + c
                                hs = slice(64 * h, 64 * h + 64)
                                mm(Ups[:, hs], XTt[:, hc, :], Xn[:, hs], True, True, ["XTt%d" % hc, "Xn"], ["pb2"])
                            K.op("dve", lambda e: e.tensor_copy(out=Ub[:, :], in_=Ups), reads=["pb2"], writes=["Ub"])
                            for h in range(0 if so else 2):
                                hc = h * NCH + c
                                hs = slice(64 * h, 64 * h + 64)
                                mm(yps[:, c, h, :], Hb[:, p, :], R2[:, c, h, :], True, False, ["Hb", "R2"], ["yps%d" % (c // 2)])
                                mm(yps[:, c, h, :], Ub[:, hs], MAt[:, hc, 128:256], False, False, ["Ub", "MAt%d" % hc], ["yps%d" % (c // 2)])
                                mm(yps[:, c, h, :], Vt[:, c, hs], AKt[:, hc, 128:256], False, True, ["Vt", "AKt%d" % hc], ["yps%d" % (c // 2)])
                            mm(pb1[:, 0:128], BKt[:, c, 0, :], Ub[:, :], True, False, ["BKt", "Ub"], ["pb1"])
                            mm(pb1[:, 0:128], BKt[:, c, 1, :], Vt[:, c, :], False, True, ["BKt", "Vt"], ["pb1"])
                            for h in range(2):
                                hp = slice(64 * h, 64 * h + 64)
                                K.op("dve", lambda e, hp=hp, h=h, ptot=ptot: e.scalar_tensor_tensor(
                                    out=Hf[hp, p, :], in0=pbank[1][hp, 64 * h:64 * h + 64], scalar=ptot[hp, :], in1=H0P[hp, :],
                                    op0=ALU.mult, op1=ALU.add), reads=["pb1", "H0P", SN(13)], writes=["Hf"])
                            act(Hb[:, p, :], Hf[:, p, :], AF.Copy, ["Hf"], ["Hb"])
                        ck(7)
                        if so:
                            continue
                        for h in range(2):
                            act(c3(ysb[:, h, :]), yps[:, :, h, :], AF.Copy, ["yps0", "yps1"], ["ysb"])
                        mm(pb0, sel0, ysb[:, 0, :], True, False, ["cf", "ysb"], ["pb0"])
                        mm(pb0, sel1, ysb[:, 1, :], False, True, ["cf", "ysb"], ["pb0"])
                        act(S(3), pb0, AF.Copy, ["pb0"], [SN(3)])
                        mm(pb1, blockavg, S(3), True, True, ["cf", SN(3)], ["pb1"])
                        tt("dve", S(4), S(3), pb1, ALU.subtract, [SN(3), "pb1"], [SN(4)])
                        tt("dve", S(5), S(4), S(4), ALU.mult, [SN(4)], [SN(5)])
                        mm(pb0, blockavg, S(5), True, True, ["cf", SN(5)], ["pb0"])
                        K.op("dve", lambda e: e.tensor_scalar(out=S(6), in0=pb0, scalar1=GN_EPS, scalar2=None, op0=ALU.add),
                             reads=["pb0"], writes=[SN(6)])
                        act(S(6), S(6), AF.Sqrt, [SN(6)], [SN(6)])
                        K.op("dve", lambda e: e.reciprocal(out=S(6), in_=S(6)), reads=[SN(6)], writes=[SN(6)])
                        tt("dve", S(7), S(4), S(6), ALU.mult, [SN(4), SN(6)], [SN(7)])
                        K.op("dve", lambda e: e.tensor_scalar(out=S(7), in0=S(7), scalar1=pcol(par, "lnw", p), scalar2=pcol(par, "lnb", p),
                                                              op0=ALU.mult, op1=ALU.add), reads=[SN(7), "par"], writes=[SN(7)])
                        tt("dve", S(7), S(7), S(16), ALU.add, [SN(7), SN(16)], [SN(7)])
                        if pas == 0:
                            K.dma("pool", "yst", yscr[p, :, t0:t0 + BT], S(7), reads=[SN(7)], writes=["yscr"])
                        else:
                            K.dma("pool", "yld", S(8), yscr[p, :, t0:t0 + BT], reads=["yscr"], writes=[SN(8)])
                            tt("dve", S(7), S(7), S(8), ALU.add, [SN(7), SN(8)], [SN(7)])
                            tt("dve", yg[:, p, :], S(7), S(17), ALU.mult, [SN(7), SN(17)], ["yg"])
                    if pas == 0 or so:
                        continue
                    for q in range(8):
                        wt, wn = w_next()
                        inproj(wt, wn, 0, 128, pb0, "pb0", False)
                        act(S(3), pb0, AF.Copy, ["pb0"], [SN(3)])
                        hapC = inproj(wt, wn, 128, 128, pb1, "pb1", True)
                        act(S(4), pb1, AF.Copy, ["pb1"], [SN(4)])
                        act(hal[:, 0:2], hapC, AF.Copy, ["pb2"], ["hal"])
                        wt, wn = w_next()
                        hapH = inproj(wt, wn, 0, 128, pb0, "pb0", True)
                        tt("dve", uf[:, 1:BT + 1], S(4), pb0, ALU.mult, [SN(4), "pb0"], ["zf"])
                        tt("dve", uf[:, 0:1], hal[:, 0:1], hapH[:, 0:1], ALU.mult, ["hal", "pb2"], ["zf"])
                        tt("dve", uf[:, BT + 1:BT + 2], hal[:, 1:2], hapH[:, 1:2], ALU.mult, ["hal", "pb2"], ["zf"])
                        inproj(wt, wn, 128, 128, pb1, "pb1", False)
                        act(S(7), pb1, AF.Sigmoid, ["pb1"], [SN(7)])
                        tt("dve", S(7), S(7), pb1, ALU.mult, [SN(7), "pb1"], [SN(7)])
                        K.op("dve", lambda e: e.tensor_scalar(out=S(5), in0=uf[:, 0:BT], scalar1=pcol(par, "cw0", q), scalar2=None, op0=ALU.mult),
                             reads=["zf", "par"], writes=[SN(5)])
                        K.op("dve", lambda e: e.scalar_tensor_tensor(out=S(5), in0=uf[:, 1:BT + 1], scalar=pcol(par, "cw1", q), in1=S(5),
                                                                     op0=ALU.mult, op1=ALU.add), reads=["zf", "par", SN(5)], writes=[SN(5)])
                        K.op("dve", lambda e: e.scalar_tensor_tensor(out=S(5), in0=uf[:, 2:BT + 2], scalar=pcol(par, "cw2", q), in1=S(5),
                                                                     op0=ALU.mult, op1=ALU.add), reads=["zf", "par", SN(5)], writes=[SN(5)])
                        tt("dve", S(6), S(3), S(5), ALU.mult, [SN(3), SN(5)], [SN(6)])
                        tt("dve", ybg[:, q, :], S(6), S(7), ALU.mult, [SN(6), SN(7)], ["ybg"])
                    for j in range(8):
                        K.dma("sp", "wab", f2(wabt[:, :, :, :].rearrange("p a k n -> p a (k n)")), swab[j, :, :], reads=["wscr"], writes=["wab"])
                        wt, wn = w_next()
                        for g2 in range(2):
                            inproj(wt, wn, g2 * 128, 128, pbank[g2][:, :], "pb%d" % g2, False)
                            act(S(3 + g2), pbank[g2][:, :], AF.Sigmoid, ["pb%d" % g2, "par"], [SN(3 + g2)], bias=pcol(par, "gba", 2 * j + g2))
                        wt, wn = w_next()
                        for g2 in range(2):
                            inproj(wt, wn, g2 * 128, 128, pbank[g2][:, :], "pb%d" % g2, False)
                            act(S(5 + g2), pbank[g2][:, :], AF.Sigmoid, ["pb%d" % g2, "par"], [SN(5 + g2)], bias=pcol(par, "gbb", 2 * j + g2))
                        for g2 in range(2):
                            for kc in range(8):
                                mm(pb0, wabt[:, 0, kc, g2 * 128:(g2 + 1) * 128], yg[:, kc, :], kc == 0, kc == 7, ["wab", "yg"], ["pb0"])
                            tt("dve", S(7), pb0, S(3 + g2), ALU.mult, ["pb0", SN(3 + g2)], [SN(7)])
                            for kc in range(8):
                                mm(pb1, wabt[:, 1, kc, g2 * 128:(g2 + 1) * 128], ybg[:, kc, :], kc == 0, kc == 7, ["wab", "ybg"], ["pb1"])
                            tt("dve", S(8), pb1, S(5 + g2), ALU.mult, ["pb1", SN(5 + g2)], [SN(8)])
                            tt("dve", merged[:, 2 * j + g2, :], S(7), S(8), ALU.add, [SN(7), SN(8)], ["merged"])
                    def res(t_):
                        return pool[:, 4 * t_:4 * t_ + 4, :].rearrange("p s t -> p (s t)")

                    def resn(t_):
                        return [SN(4 * t_ + k) for k in range(4)]

                    for n in range(8):
                        wt, wn = w_next()
                        for t_ in range(4):
                            pbx = pbank[t_ % 2][:, 0:256]
                            for kc in range(NKC):
                                mm(pbx, merged[:, kc, t_ * 128:(t_ + 1) * 128], wt[:, kc, :], kc == 0, kc == NKC - 1, ["merged", wn], ["pb%d" % (t_ % 2)])
                            act(res(t_)[:, n * 256:(n + 1) * 256], pbx, AF.Copy, ["pb%d" % (t_ % 2)], resn(t_))
                    for t_ in range(4):
                        K.dma("sp", "xl", xt[:, :], x_d[t0 + t_ * 128:t0 + (t_ + 1) * 128, :], writes=["xt"])
                        tt("dve", res(t_), res(t_), xt[:, :], ALU.add, resn(t_) + ["xt"], resn(t_))
                        rstd_rows(res(t_), resn(t_), 128, 4, 5)
                        K.op("dve", lambda e, t_=t_: e.scalar_tensor_tensor(out=res(t_), in0=res(t_), scalar=st[:, 5:6], in1=fnw[:, :],
                                                                            op0=ALU.mult, op1=ALU.mult), reads=resn(t_) + ["st", "fnw"], writes=resn(t_))
                        K.dma("pool", "out%d" % t_, y_d[t0 + t_ * 128:t0 + (t_ + 1) * 128, :], res(t_), reads=resn(t_), writes=["ydram"])
        except _Stop:
            pass
        for eng in ["pe", "act", "dve", "pool", "sp"]:
            K.final_wait(eng)
    return nc


def _host_layout(inp, mirror=False):
    f = np.float32
    inp = dict(inp)
    W = np.asarray(inp["w_in"][0], f)
    mu_ = np.asarray(inp["mu_shift"][0], f)
    if mirror:
        perm = np.arange(NCOLS)
        perm[3072:3168], perm[3168:3264] = np.arange(3168, 3264), np.arange(3072, 3168)
        perm[3264:3360], perm[3360:3456] = np.arange(3360, 3456), np.arange(3264, 3360)
        W = W[:, perm]
        mu_ = mu_[perm[:3456]]
        for k_ in ["w0", "w_up", "a0", "a_up"]:
            inp[k_] = np.asarray(inp[k_])[:, ::-1]
        inp["conv_w"] = np.asarray(inp["conv_w"])[:, ::-1]
    inp["mu_shift"] = mu_[None]

    def blob(cols):
        Wc = W[:, cols]
        n = Wc.shape[1]
        return np.ascontiguousarray(Wc.reshape(16, 128, n).transpose(1, 0, 2).reshape(128, 16 * n))

    ar = np.arange
    out = {}
    Wl = np.zeros((2048, 512), f)
    for g in range(4):
        Wl[:, g * 128:g * 128 + 96] = W[:, 3072 + g * 96:3072 + (g + 1) * 96]
    out["w_l"] = np.ascontiguousarray(Wl.reshape(16, 128, 512).transpose(1, 0, 2).reshape(128, 16 * 512))
    wA, wB, wM = [], [], []
    for p in range(8):
        wA.append(blob(np.concatenate([ar(p * 128, p * 128 + 128), 1024 + ar(p * 128, p * 128 + 128)])))
        wA.append(blob(np.concatenate([2048 + ar(p * 128, p * 128 + 128), 3456 + ar(p * 128, p * 128 + 128)])))
        wB.append(blob(np.concatenate([4480 + ar(p * 128, p * 128 + 128), 5504 + ar(p * 128, p * 128 + 128)])))
        wB.append(blob(np.concatenate([6528 + ar(p * 128, p * 128 + 128), 7552 + ar(p * 128, p * 128 + 128)])))
        wM.append(blob(8576 + ar(p * 256, p * 256 + 256)))
        wM.append(blob(10624 + ar(p * 256, p * 256 + 256)))
    out["w_A"] = np.stack(wA)
    out["w_B"] = np.stack(wB)
    out["w_M"] = np.stack(wM)
    wa = np.asarray(inp["w_a_out"][0], f)
    wb = np.asarray(inp["w_b_out"][0], f)
    wab = []
    for j in range(8):
        a = wa[:, j * 256:(j + 1) * 256].reshape(8, 128, 256).transpose(1, 0, 2)
        b_ = wb[:, j * 256:(j + 1) * 256].reshape(8, 128, 256).transpose(1, 0, 2)
        wab.append(np.stack([a, b_], axis=1).reshape(128, 2 * 8 * 256))
    out["w_ab"] = np.ascontiguousarray(np.stack(wab))
    wo = np.asarray(inp["w_o"][0], f)
    out["w_o"] = np.ascontiguousarray(np.stack([wo[:, n * 256:(n + 1) * 256].reshape(16, 128, 256).transpose(1, 0, 2).reshape(128, 16 * 256)
                                                for n in range(8)]))
    wup_ = np.zeros((128, 4096), f)
    wup_[0:96, :] = np.concatenate([inp["w_up"][0, 0], inp["w_up"][0, 1], inp["a_up"][0, 0], inp["a_up"][0, 1]], axis=1)
    out["w_up"] = wup_
    par = np.zeros((128, NPAR), f)

    def put(name, vec, ncol):
        par[:, PC[name]:PC[name] + ncol] = np.asarray(vec, f).reshape(ncol, 128).T

    put("w0f", inp["w0"][0, 0], 8); put("w0b", inp["w0"][0, 1], 8)
    put("a0f", inp["a0"][0, 0], 8); put("a0b", inp["a0"][0, 1], 8)
    put("kk", inp["k_k"][0], 8); put("ka", inp["k_a"][0], 8); put("rk", np.asarray(inp["r_k"][0]).reshape(1024), 8)
    put("lnw", inp["ln_w"][0], 8); put("lnb", inp["ln_b"][0], 8)
    mu = np.asarray(inp["mu_shift"][0], f)
    put("mur", mu[0:1024], 8); put("muk", mu[1024:2048], 8); put("muv", mu[2048:3072], 8)
    par[0:96, PC["mul"]:PC["mul"] + 4] = mu[3072:3456].reshape(4, 96).T
    put("gba", inp["gate_bias"][0, 0], 16); put("gbb", inp["gate_bias"][0, 1], 16)
    for j in range(3):
        put("cw%d" % j, inp["conv_w"][0, j], 8)
    put("nw", inp["norm_w"][0], 16)
    out["params"] = par
    out["fnw"] = np.ascontiguousarray(np.broadcast_to(np.asarray(inp["final_norm_w"], f)[None, :], (128, D)))
    bf = ml_dtypes.bfloat16
    i = np.arange(128)
    su = (i[:, None] < i[None, :]).astype(f)
    sl = su.T.copy()
    eye = np.eye(128, dtype=f)
    iu = su + eye
    il = sl + eye
    cb = np.zeros((128, 128 + 2 * 640), f)
    cb[:, 0:128] = eye
    cb[:, 128:128 + 640] = np.concatenate([-su, iu, su, iu, -sl], axis=1)
    cb[:, 768:768 + 640] = np.concatenate([-sl, il, sl, il, -su], axis=1)
    out["cbf"] = cb.astype(bf)
    cfa = np.zeros((128, 640), f)
    blk = np.zeros((128, 128), f)
    blk[0:64, 0:64] = 1.0
    blk[64:128, 64:128] = 1.0
    cfa[:, 0:128] = blk
    cfa[:, 128:256] = blk / 64.0
    cfa[0:64, 256:320] = np.eye(64, dtype=f)
    cfa[0:64, 384 + 64:384 + 128] = np.eye(64, dtype=f)
    cfa[:, 512:640] = 1.0
    out["cf32"] = cfa
    return out


_NC_CACHE = {}


def _flags(seq_len, NT):
    NB = NT // BT
    keepL = np.zeros(NB, np.float32)
    for b in range(1, NB):
        keepL[b] = 1.0 if (b * BT) % seq_len != 0 else 0.0
    keepR = np.zeros(NB, np.float32)
    keepR[:-1] = keepL[1:]
    fl2 = np.stack([keepL, keepR])
    flp = np.zeros((128, 2 * NB), np.float32)
    flp[:, 0::2] = keepL[None, :]
    flp[:, 1::2] = keepR[None, :]
    return flp, fl2


def run_streams(streams, seq_lens, inp, NT, NBO=None, mirrors=None):
    if (NT, NBO) not in _NC_CACHE:
        _NC_CACHE[(NT, NBO)] = build_nc(NT, NBO)
    nc = _NC_CACHE[(NT, NBO)]
    mirrors = mirrors or [False] * 8
    lays = {False: _host_layout(inp, False)}
    if any(mirrors):
        lays[True] = _host_layout(inp, True)
    in_maps = []
    for c in range(8):
        flp, fl2 = _flags(seq_lens[c], NT)
        m = dict(lays[bool(mirrors[c])])
        m["x"] = np.ascontiguousarray(streams[c], dtype=np.float32)
        m["flagsP"] = flp
        m["flags2"] = fl2
        in_maps.append(m)
    res = run_bass_kernel_spmd(nc, in_maps, core_ids=list(range(8)))
    return [res.results[c]["y"] for c in range(8)]


def kernel(**inp):
    xp = np.asarray(inp["x_prompt"], np.float32)
    xsm = np.asarray(inp["x_sample"], np.float32)
    NT, NBO = 16384, 16
    H = NT // 2
    zeros = np.zeros((H, D), np.float32)
    streams = [xp[0], xp[0][::-1], xp[1], xp[1][::-1]]
    for c in range(4):
        streams.append(np.concatenate([xsm[4 * c:4 * c + 4].reshape(H, D), zeros], axis=0))
    seq_lens = [16384] * 4 + [2048] * 4
    mirrors = [False, True, False, True, False, False, False, False]
    ys = run_streams(streams, seq_lens, inp, NT, NBO, mirrors)
    y_prompt = np.stack([np.concatenate([ys[0], ys[1][::-1]], axis=0), np.concatenate([ys[2], ys[3][::-1]], axis=0)]).astype(np.float32)
    y_sample = np.concatenate([ys[4 + c].reshape(4, 2048, D) for c in range(4)], axis=0).astype(np.float32)
    return (y_prompt, y_sample)
```

```python
import numpy as np
import ml_dtypes
from contextlib import ExitStack
import concourse.bass as bass
import concourse.mybir as mybir
from concourse.bass_utils import run_bass_kernel_spmd

F32 = mybir.dt.float32
BF16 = mybir.dt.bfloat16
AF = mybir.ActivationFunctionType
ALU = mybir.AluOpType

D = 2048
NKC = 16
NCOLS = 12672
BT = 512
NCH = BT // 128
C0 = float(np.exp(-0.5))
RMS_EPS = 1e-6
GN_EPS = 64e-5

PC = {}
_o = 0
for _n, _w in [("w0f", 8), ("w0b", 8), ("a0f", 8), ("a0b", 8), ("kk", 8), ("ka", 8), ("rk", 8), ("lnw", 8), ("lnb", 8),
               ("mur", 8), ("muk", 8), ("muv", 8), ("mul", 4), ("gba", 16), ("gbb", 16), ("cw0", 8), ("cw1", 8), ("cw2", 8),
               ("nw", 16)]:
    PC[_n] = _o
    _o += _w
NPAR = _o


class Ctx:
    def __init__(self, nc, es):
        self.nc = nc
        self.es = es
        self.E = {"pe": nc.tensor, "act": nc.scalar, "dve": nc.vector, "pool": nc.gpsimd, "sp": nc.sync}
        self.sem = {k: es.enter_context(nc.semaphore("sem_" + k)) for k in self.E}
        self.cnt = {k: 0 for k in self.E}
        self.dsem = {}
        self.dcnt = {}
        self.waited = {k: {} for k in self.E}
        self.res = {}

    def _semh(self, key):
        return self.sem[key] if key in self.sem else self.dsem[key]

    def _wait(self, eng, key, val):
        if self.waited[eng].get(key, 0) >= val:
            return
        self.waited[eng][key] = val
        self.E[eng].wait_ge(self._semh(key), val)

    def _deps(self, eng, reads, writes):
        toks = {}

        def add(t):
            if t is not None and toks.get(t[0], 0) < t[1]:
                toks[t[0]] = t[1]

        for r in reads:
            st = self.res.get(r)
            if st:
                add(st[0])
        for w in writes:
            st = self.res.get(w)
            if st:
                add(st[0])
                for k, v in st[1].items():
                    add((k, v))
        for k, v in toks.items():
            if k == eng and eng == "pe":
                continue
            self._wait(eng, k, v)

    def _commit(self, tok, reads, writes):
        for r in reads:
            st = self.res.setdefault(r, [None, {}])
            if st[1].get(tok[0], 0) < tok[1]:
                st[1][tok[0]] = tok[1]
        for w in writes:
            self.res[w] = [tok, {}]

    def op(self, eng, fn, reads=(), writes=()):
        self._deps(eng, reads, writes)
        inst = fn(self.E[eng])
        self.cnt[eng] += 1
        inst.then_inc(self.sem[eng], 1)
        self._commit((eng, self.cnt[eng]), reads, writes)

    def dma(self, q, semkey, out, in_, reads=(), writes=()):
        if semkey not in self.dsem:
            self.dsem[semkey] = self.es.enter_context(self.nc.semaphore("dsem_" + semkey))
            self.dcnt[semkey] = 0
        self._deps(q, reads, writes)
        inst = self.E[q].dma_start(out=out, in_=in_)
        self.dcnt[semkey] += 16
        inst.then_inc(self.dsem[semkey], 16)
        self._commit((semkey, self.dcnt[semkey]), reads, writes)

    def final_wait(self, eng="sp"):
        for k in list(self.sem) + list(self.dsem):
            v = self.cnt[k] if k in self.cnt else self.dcnt[k]
            if v > 0:
                self._wait(eng, k, v)


class _Stop(Exception):
    pass


def build_nc(NT, NBO=None, STOP=0):
    NB = NT // BT
    if NBO is None:
        NBO = NB
    NTO = NBO * BT

    def ck(n):
        if STOP == n:
            raise _Stop()
    nc = bass.Bass("TRN2", target_bir_lowering=False)
    dt = nc.dram_tensor
    x_d = dt("x", [NT, D], F32, kind="ExternalInput").ap()
    wl_d = dt("w_l", [128, NKC * 512], F32, kind="ExternalInput").ap()
    wA_d = dt("w_A", [16, 128, NKC * 256], F32, kind="ExternalInput").ap()
    wB_d = dt("w_B", [16, 128, NKC * 256], F32, kind="ExternalInput").ap()
    wM_d = dt("w_M", [16, 128, NKC * 256], F32, kind="ExternalInput").ap()
    wab_d = dt("w_ab", [8, 128, 2 * 8 * 256], F32, kind="ExternalInput").ap()
    wo_d = dt("w_o", [8, 128, NKC * 256], F32, kind="ExternalInput").ap()
    wup_d = dt("w_up", [128, 4 * 1024], F32, kind="ExternalInput").ap()
    par_d = dt("params", [128, NPAR], F32, kind="ExternalInput").ap()
    fnw_d = dt("fnw", [128, D], F32, kind="ExternalInput").ap()
    flp_d = dt("flagsP", [128, 2 * NB], F32, kind="ExternalInput").ap()
    fl2_d = dt("flags2", [2, NB], F32, kind="ExternalInput").ap()
    cbf_d = dt("cbf", [128, 128 + 2 * 640], BF16, kind="ExternalInput").ap()
    cf_d = dt("cf32", [128, 5 * 128], F32, kind="ExternalInput").ap()
    y_d = dt("y", [NTO, D], F32, kind="ExternalOutput").ap()
    swl = dt("s_wl", [128, NKC * 512], BF16, kind="Internal").ap()
    swA = dt("s_wA", [16, 128, NKC * 256], BF16, kind="Internal").ap()
    swB = dt("s_wB", [16, 128, NKC * 256], BF16, kind="Internal").ap()
    swM = dt("s_wM", [16, 128, NKC * 256], BF16, kind="Internal").ap()
    swab = dt("s_wab", [8, 128, 2 * 8 * 256], BF16, kind="Internal").ap()
    swo = dt("s_wo", [8, 128, NKC * 256], BF16, kind="Internal").ap()
    yscr = dt("s_yf", [8, 128, NTO], F32, kind="Internal").ap()

    es = ExitStack()
    with es:
        K = Ctx(nc, es)

        def sb(name, shape, dtype):
            return es.enter_context(nc.sbuf_tensor("sb_" + name, shape, dtype))

        def ps(name, shape, dtype):
            return es.enter_context(nc.psum_tensor("ps_" + name, shape, dtype))

        cbf = sb("cbf", [128, 128 + 2 * 640], BF16)
        cf = sb("cf", [128, 5 * 128], F32)
        par = sb("par", [128, NPAR], F32)
        par1 = sb("par1", [128, NPAR], F32)
        par2 = sb("par2", [128, NPAR], F32)
        fnw = sb("fnw", [128, D], F32)
        flp = sb("flp", [128, 2 * NB], F32)
        fl2 = sb("fl2", [2, NB], F32)
        wup = sb("wup", [128, 4 * 1024], BF16)
        K.dma("sp", "c0", cbf[:, :], cbf_d[:, :], writes=["cbf"])
        K.dma("sp", "c0", cf[:, :], cf_d[:, :], writes=["cf"])
        K.dma("sp", "c0", par[:, :], par_d[:, :], writes=["par"])
        K.dma("sp", "c0", fnw[:, :], fnw_d[:, :], writes=["fnw"])
        K.dma("sp", "c0", flp[:, :], flp_d[:, :], writes=["flp"])
        K.dma("sp", "c0", fl2[:, :], fl2_d[:, :], writes=["fl2"])
        for _nm in ["cbf", "cf", "par", "fnw", "flp", "fl2"]:
            K.res[_nm][0] = ("c0", K.dcnt["c0"])
        ident = cbf[:, 0:128]

        def masks(d):
            o = 128 + d * 640
            return cbf[:, o:o + 256], cbf[:, o + 256:o + 512], cbf[:, o + 512:o + 640]

        blockones = cf[:, 0:128]
        blockavg = cf[:, 128:256]
        sel0 = cf[0:64, 256:384]
        sel1 = cf[0:64, 384:512]
        ones128 = cf[:, 512:640]
        K.op("dve", lambda e: e.tensor_scalar(out=par1[:, :], in0=par[:, :], scalar1=-1.0, scalar2=1.0,
                                              op0=ALU.mult, op1=ALU.add), reads=["par"], writes=["par1"])
        K.op("dve", lambda e: e.tensor_scalar(out=par2[:, :], in0=par[:, :], scalar1=0.5, scalar2=None,
                                              op0=ALU.mult), reads=["par"], writes=["par2"])

        def pcol(t, name, i, np_=128):
            return t[0:np_, PC[name] + i:PC[name] + i + 1]

        with nc.sbuf_tensor("pst_f", [128, 2, 4096], F32) as stg_f, nc.sbuf_tensor("pst_b", [128, 2, 4096], BF16) as stg_b:
            jobs = [(wl_d[:, :], swl[:, :], NKC * 512)]
            for src, dst, n, F in [(wA_d, swA, 16, NKC * 256), (wB_d, swB, 16, NKC * 256), (wM_d, swM, 16, NKC * 256),
                                   (wab_d, swab, 8, 4096), (wo_d, swo, 8, NKC * 256)]:
                for i in range(n):
                    jobs.append((src[i, :, :], dst[i, :, :], F))
            ci = 0
            for (src, dst, F) in jobs:
                for c0 in range(0, F, 4096):
                    w = min(4096, F - c0)
                    s = ci % 2
                    K.dma("sp", "pcl%d" % s, stg_f[:, s, 0:w], src[:, c0:c0 + w], writes=["stg_f%d" % s])
                    eng = ["dve", "act", "pool"][ci % 3]
                    if eng == "act":
                        K.op("act", lambda e, s=s, w=w: e.activation(out=stg_b[:, s, 0:w], in_=stg_f[:, s, 0:w], func=AF.Copy),
                             reads=["stg_f%d" % s], writes=["stg_b%d" % s])
                    else:
                        K.op(eng, lambda e, s=s, w=w: e.tensor_copy(out=stg_b[:, s, 0:w], in_=stg_f[:, s, 0:w]),
                             reads=["stg_f%d" % s], writes=["stg_b%d" % s])
                    K.dma("pool", "pcs%d" % s, dst[:, c0:c0 + w], stg_b[:, s, 0:w], reads=["stg_b%d" % s], writes=["wscr"])
                    ci += 1
            K.dma("sp", "pcl0", stg_f[:, 0, :], wup_d[:, :], writes=["stg_f0"])
            K.op("dve", lambda e: e.tensor_copy(out=wup[:, :], in_=stg_f[:, 0, :]), reads=["stg_f0"], writes=["wup"])
            for eng in ["pe", "act", "dve", "pool", "sp"]:
                K.final_wait(eng)

        NS = 22
        pool = sb("pool", [128, NS, BT], F32)
        xt = sb("xt", [128, D], F32)
        xs = sb("xs", [128, D], BF16)
        st = sb("stats", [128, 8], F32)
        hT = sb("hT", [128, NKC, BT], BF16)
        hTh = sb("hTh", [128, NKC, 2], BF16)
        wbf = [sb("wbf%d" % i, [128, NKC * 512], BF16) for i in range(2)]
        wabt = sb("wab", [128, 2, 8, 256], BF16)
        twd = sb("twd", [128, BT], BF16)
        adb = sb("adb", [128, BT], BF16)
        zf = sb("zf", [128, BT + 2], F32)
        KR = sb("KR", [128, NCH, 2, 128], BF16)
        R2 = sb("R2", [128, NCH, 2, 128], BF16)
        BTt = sb("BTt", [128, BT], BF16)
        KTt = sb("KTt", [128, BT], BF16)
        vbf = sb("vbf", [128, BT], BF16)
        Vt = sb("Vt", [128, NCH, 128], BF16)
        BKt = sb("BKt", [128, NCH, 2, 128], BF16)
        MAt = sb("MAt", [128, 2 * NCH, 256], BF16)
        AKt = sb("AKt", [128, 2 * NCH, 256], BF16)
        XTt = sb("XTt", [128, 2 * NCH, 128], BF16)
        NM = sb("NM", [128, NCH, 2, 256], BF16)
        XTw = sb("XTw", [128, NCH, 2, 128], BF16)
        Xn = sb("Xn", [128, 128], BF16)
        Ub = sb("Ub", [128, 128], BF16)
        Hf = sb("Hf", [128, 8, 64], F32)
        Hb = sb("Hb", [128, 8, 64], BF16)
        H0P = sb("H0P", [128, 64], F32)
        ysb = sb("ysb", [64, 2, BT], F32)
        yg = sb("yg", [128, 8, BT], BF16)
        ybg = sb("ybg", [128, 8, BT], BF16)
        merged = sb("merged", [128, NKC, BT], BF16)
        uf = zf
        hal = sb("hal", [128, 4], F32)

        pbank = [ps("pb%d" % i, [128, 512], F32) for i in range(5)]
        ptr = ps("ptr", [128, 1024], BF16)
        yps = ps("yps", [64, NCH, 2, 128], F32)

        def S(i):
            return pool[:, i, :]

        def SN(i):
            return "pool%d" % i

        K.op("pool", lambda e: e.memset(R2[:, :, :, :].rearrange("p a b c -> p (a b c)"), 0.0), writes=["R2"])

        wq = []
        wstate = {"issued": 0, "used": 0}

        def w_issue():
            i = wstate["issued"]
            s = i % 2
            src, ncol = wq[i]
            K.dma("sp", "wl%d" % s, wbf[s][:, 0:NKC * ncol], src, reads=["wscr"], writes=["wbf%d" % s])
            wstate["issued"] += 1

        def w_next():
            i = wstate["used"]
            while wstate["issued"] <= min(i + 1, len(wq) - 1):
                w_issue()
            wstate["used"] += 1
            ncol = wq[i][1]
            return wbf[i % 2][:, 0:NKC * ncol].rearrange("p (k n) -> p k n", n=ncol), "wbf%d" % (i % 2)

        for pas in range(2):
            for bi in (list(range(NBO)) if pas == 0 else list(range(NB - 1, -1, -1))):
                wq.append((swl[:, :], 512))
                for p in range(16):
                    wq.append((swA[p, :, :], 256))
                if pas == 1 and bi < NBO:
                    for q in range(16):
                        wq.append((swB[q, :, :], 256))
                    for j in range(16):
                        wq.append((swM[j, :, :], 256))
                    for n in range(8):
                        wq.append((swo[n, :, :], 256))

        hps_col = [0]

        def inproj(wt, wname, col0, M, pmain, pmain_name, halo):
            hap = None
            if halo:
                c = hps_col[0] % 64
                hps_col[0] += 1
                hap = pbank[2][0:M, 4 * c:4 * c + 2]
            for kc in range(NKC):
                K.op("pe", lambda e, kc=kc: e.matmul(pmain, wt[:, kc, col0:col0 + M], hT[:, kc, :], start=(kc == 0), stop=(kc == NKC - 1)),
                     reads=[wname, "hT"], writes=[pmain_name])
            ck(22)
            if halo:
                for kc in range(NKC):
                    K.op("pe", lambda e, kc=kc: e.matmul(hap, wt[:, kc, col0:col0 + M], hTh[:, kc, :], start=(kc == 0), stop=(kc == NKC - 1)),
                         reads=[wname, "hTh"], writes=["pb2"])
            ck(23)
            return hap

        def shiftmix(M, pmain, pmain_name, hap, c1, c2, out_i):
            K.op("act", lambda e: e.activation(out=zf[0:M, 1:BT + 1], in_=pmain, func=AF.Copy), reads=[pmain_name], writes=["zf"])
            K.op("act", lambda e: e.activation(out=zf[0:M, 0:1], in_=hap[:, 0:1], func=AF.Copy), reads=["pb2"], writes=["zf"])
            K.op("act", lambda e: e.activation(out=zf[0:M, BT + 1:BT + 2], in_=hap[:, 1:2], func=AF.Copy), reads=["pb2"], writes=["zf"])
            ck(24)
            K.op("dve", lambda e: e.tensor_scalar(out=S(0)[0:M, :], in0=zf[0:M, 1:BT + 1], scalar1=c1, scalar2=None, op0=ALU.mult),
                 reads=["zf", "par1"], writes=[SN(0)])
            ck(25)
            K.op("dve", lambda e: e.tensor_tensor(out=S(1)[0:M, :], in0=zf[0:M, 0:BT], in1=zf[0:M, 2:BT + 2], op=ALU.add),
                 reads=["zf"], writes=[SN(1)])
            K.op("dve", lambda e: e.scalar_tensor_tensor(out=S(out_i)[0:M, :], in0=S(1)[0:M, :], scalar=c2, in1=S(0)[0:M, :],
                                                         op0=ALU.mult, op1=ALU.add),
                 reads=[SN(0), SN(1), "par2"], writes=[SN(out_i)])

        def rstd_rows(src, srcnames, npart, c_ss, c_out, flagcol=None):
            K.op("pool", lambda e: e.memset(st[0:npart, c_ss:c_ss + 1], 0.0), writes=["st"])
            K.op("act", lambda e: e.activation(out=xs[0:npart, :], in_=src, func=AF.Square, accum_out=st[0:npart, c_ss:c_ss + 1]),
                 reads=list(srcnames) + ["st"], writes=["xs", "st"])
            K.op("dve", lambda e: e.tensor_scalar(out=st[0:npart, c_out:c_out + 1], in0=st[0:npart, c_ss:c_ss + 1],
                                                  scalar1=1.0 / D, scalar2=RMS_EPS, op0=ALU.mult, op1=ALU.add),
                 reads=["st"], writes=["st"])
            K.op("act", lambda e: e.activation(out=st[0:npart, c_out:c_out + 1], in_=st[0:npart, c_out:c_out + 1], func=AF.Sqrt),
                 reads=["st"], writes=["st"])
            K.op("dve", lambda e: e.reciprocal(out=st[0:npart, c_out:c_out + 1], in_=st[0:npart, c_out:c_out + 1]),
                 reads=["st"], writes=["st"])
            if flagcol is not None:
                K.op("dve", lambda e: e.tensor_scalar(out=st[0:npart, c_out:c_out + 1], in0=st[0:npart, c_out:c_out + 1],
                                                      scalar1=flagcol, scalar2=None, op0=ALU.mult),
                     reads=["st", "fl2"], writes=["st"])

        def build_hT(b):
            t0 = b * BT
            for i in range(BT // 128):
                K.dma("sp", "xl", xt[:, :], x_d[t0 + i * 128:t0 + (i + 1) * 128, :], writes=["xt"])
                rstd_rows(xt[:, :], ["xt"], 128, 0, 1)
                K.op("dve", lambda e: e.tensor_scalar(out=xs[:, :], in0=xt[:, :], scalar1=st[:, 1:2], scalar2=None, op0=ALU.mult),
                     reads=["xt", "st"], writes=["xs"])
                for half in range(2):
                    for k8 in range(8):
                        kc = half * 8 + k8
                        K.op("pe", lambda e, kc=kc, k8=k8: e.transpose(ptr[:, k8 * 128:(k8 + 1) * 128], xs[:, kc * 128:(kc + 1) * 128], ident),
                             reads=["xs", "cbf"], writes=["ptr"])
                    for k8 in range(8):
                        kc = half * 8 + k8
                        K.op("dve", lambda e, kc=kc, k8=k8, i=i: e.tensor_scalar(out=hT[:, kc, i * 128:(i + 1) * 128],
                                                                                 in0=ptr[:, k8 * 128:(k8 + 1) * 128],
                                                                                 scalar1=pcol(par, "nw", kc), scalar2=None, op0=ALU.mult),
                             reads=["ptr", "par"], writes=["hT"])
            tl = max(t0 - 1, 0)
            tr = min(t0 + BT, NT - 1)
            K.dma("sp", "xhl", xt[0:1, :], x_d[tl:tl + 1, :], writes=["xt"])
            K.dma("sp", "xhl", xt[1:2, :], x_d[tr:tr + 1, :], writes=["xt"])
            rstd_rows(xt[0:2, :], ["xt"], 2, 2, 3, flagcol=fl2[0:2, b:b + 1])
            K.op("dve", lambda e: e.tensor_scalar(out=xs[0:2, :], in0=xt[0:2, :], scalar1=st[0:2, 3:4], scalar2=None, op0=ALU.mult),
                 reads=["xt", "st"], writes=["xs"])
            for kc in range(NKC):
                K.op("pe", lambda e, kc=kc: e.transpose(ptr[:, 2 * kc:2 * kc + 2], xs[0:2, kc * 128:(kc + 1) * 128], ident[0:2, 0:2]),
                     reads=["xs", "cbf"], writes=["ptr"])
            for kc in range(NKC):
                K.op("dve", lambda e, kc=kc: e.tensor_scalar(out=hTh[:, kc, :], in0=ptr[:, 2 * kc:2 * kc + 2], scalar1=pcol(par, "nw", kc),
                                                             scalar2=None, op0=ALU.mult),
                     reads=["ptr", "par"], writes=["hTh"])

        def c3(ap):
            return ap.rearrange("p (c t) -> p c t", t=128)

        def f2(ap):
            return ap.rearrange("p a t -> p (a t)")

        def tt(eng, out, in0, in1, op, reads, writes):
            K.op(eng, lambda e: e.tensor_tensor(out=out, in0=in0, in1=in1, op=op), reads=reads, writes=writes)

        def act(out, in_, func, reads, writes, **kw):
            K.op("act", lambda e: e.activation(out=out, in_=in_, func=func, **kw), reads=reads, writes=writes)

        def mm(out, lhsT, rhs, start, stop, reads, writes):
            K.op("pe", lambda e: e.matmul(out, lhsT, rhs, start=start, stop=stop), reads=reads, writes=writes)

        pb0, pb1, pb2, pb3, pb4 = pbank[0][:, :], pbank[1][:, :], pbank[2], pbank[3], pbank[4]
        Xps = pb2[:, 256:384]
        Ups = pb2[:, 384:512]
        ps1 = pb3[:, 0:256]
        ps2 = pb2[:, 0:256]
        ps3 = pb4[:, 0:128]
        sqN = pb4[:, 128:256]
        sqM = pbank[0][:, 0:128]
        xu = pbank[1][:, 0:128]

        try:
            for pas in range(2):
                d = pas
                mMA, mAK, mN = masks(d)
                K.op("pool", lambda e: e.memset(Hf[:, :, :], 0.0), writes=["Hf"])
                K.op("pool", lambda e: e.memset(Hb[:, :, :], 0.0), writes=["Hb"])
                blocks = list(range(NBO)) if pas == 0 else list(range(NB - 1, -1, -1))
                for b in blocks:
                    so = (pas == 1 and b >= NBO)
                    t0 = b * BT
                    ck(1)
                    build_hT(b)
                    fcol = flp[:, 2 * b + d:2 * b + d + 1]
                    K.op("dve", lambda e: e.tensor_scalar(out=Hf[:, :, :], in0=Hf[:, :, :], scalar1=fcol, scalar2=None, op0=ALU.mult),
                         reads=["Hf", "flp"], writes=["Hf"])
                    K.op("pool", lambda e: e.tensor_copy(out=Hb[:, :, :], in_=Hf[:, :, :]), reads=["Hf"], writes=["Hb"])
                    ck(21)
                    ck(2)
                    wt, wn = w_next()
                    for li, (cc, dst, fn) in enumerate([(d * 128, twd, AF.Tanh), (256 + d * 128, adb, AF.Copy)]):
                        pm = pbank[li][:, :]
                        hap = inproj(wt, wn, cc, 128, pm, "pb%d" % li, True)
                        mi = (0 if li == 0 else 2) + d
                        shiftmix(128, pm, "pb%d" % li, hap, pcol(par1, "mul", mi), pcol(par2, "mul", mi), 2)
                        if li == 0:
                            act(S(2), S(2), AF.Sigmoid, [SN(2)], [SN(2)], scale=2.0)
                            K.op("dve", lambda e: e.tensor_scalar(out=twd[:, :], in0=S(2), scalar1=2.0, scalar2=-1.0,
                                                                  op0=ALU.mult, op1=ALU.add), reads=[SN(2)], writes=["lora0"])
                        else:
                            K.op("dve", lambda e: e.tensor_copy(out=adb[:, :], in_=S(2)), reads=[SN(2)], writes=["lora1"])
                    def in_steps(pp):
                        st_ = pp % 2
                        R2_, K2_, V2_, G2_ = (3, 4, 5, 17) if st_ == 0 else (18, 19, 20, 21)
                        box = {}

                        def g0():
                            box["w"] = w_next()
                            if not so:
                                box["h"] = inproj(box["w"][0], box["w"][1], 0, 128, pbank[3][:, :], "pb3", True)

                        def s0():
                            if not so:
                                shiftmix(128, pbank[3][:, :], "pb3", box["h"], pcol(par1, "mur", pp), pcol(par2, "mur", pp), R2_)

                        def g1():
                            box["h"] = inproj(box["w"][0], box["w"][1], 128, 128, pbank[4][:, :], "pb4", True)

                        def s1():
                            shiftmix(128, pbank[4][:, :], "pb4", box["h"], pcol(par1, "muk", pp), pcol(par2, "muk", pp), K2_)

                        def g2():
                            box["w"] = w_next()
                            box["h"] = inproj(box["w"][0], box["w"][1], 0, 128, pbank[3][:, :], "pb3", True)

                        def s2():
                            shiftmix(128, pbank[3][:, :], "pb3", box["h"], pcol(par1, "muv", pp), pcol(par2, "muv", pp), V2_)

                        def g3():
                            if pas == 1 and not so:
                                inproj(box["w"][0], box["w"][1], 128, 128, pbank[4][:, :], "pb4", False)

                        def s3():
                            if pas == 1 and not so:
                                act(S(G2_), pbank[4][:, :], AF.Sigmoid, ["pb4"], [SN(G2_)])
                                tt("dve", S(G2_), S(G2_), pbank[4][:, :], ALU.mult, [SN(G2_), "pb4"], [SN(G2_)])

                        return [g0, s0, g1, s1, g2, s2, g3, s3]

                    for f_ in in_steps(0):
                        f_()
                    for p in range(8):
                        ck(3)
                        R_, K_, V_, G_ = (3, 4, 5, 17) if p % 2 == 0 else (18, 19, 20, 21)
                        nxt = in_steps(p + 1) if p < 7 else [lambda: None] * 8
                        ck(4)
                        mm(pb0, wup[:, d * 1024 + p * 128:d * 1024 + (p + 1) * 128], twd[:, :], True, True, ["wup", "lora0"], ["pb0"])
                        act(S(6), pb0, AF.Sigmoid, ["pb0", "par"], [SN(6)], bias=pcol(par, "w0b" if d else "w0f", p))
                        mm(pb1, wup[:, (2 + d) * 1024 + p * 128:(2 + d) * 1024 + (p + 1) * 128], adb[:, :], True, True, ["wup", "lora1"], ["pb1"])
                        act(S(7), pb1, AF.Sigmoid, ["pb1", "par"], [SN(7)], bias=pcol(par, "a0b" if d else "a0f", p))
                        K.op("dve", lambda e: e.tensor_scalar(out=S(8), in0=S(K_), scalar1=pcol(par, "kk", p), scalar2=None, op0=ALU.mult),
                             reads=[SN(K_), "par"], writes=[SN(8)])
                        tt("dve", S(9), S(8), S(8), ALU.mult, [SN(8)], [SN(9)])
                        mm(pb0, blockones, S(9), True, True, ["cf", SN(9)], ["pb0"])
                        K.op("dve", lambda e: e.tensor_scalar(out=S(10), in0=S(7), scalar1=-1.0, scalar2=pcol(par, "ka", p),
                                                              op0=ALU.add, op1=ALU.mult), reads=[SN(7), "par"], writes=[SN(10)])
                        K.op("dve", lambda e: e.scalar_tensor_tensor(out=S(10), in0=S(10), scalar=1.0, in1=S(K_), op0=ALU.add, op1=ALU.mult),
                             reads=[SN(10), SN(K_)], writes=[SN(10)])
                        if not so:
                            K.op("dve", lambda e: e.scalar_tensor_tensor(out=S(12), in0=S(R_), scalar=pcol(par, "rk", p), in1=S(10),
                                                                         op0=ALU.mult, op1=ALU.mult), reads=[SN(R_), SN(10), "par"], writes=[SN(12)])
                            mm(pb1, blockones, S(12), True, True, ["cf", SN(12)], ["pb1"])
                        act(vbf[:, :], S(V_), AF.Copy, [SN(V_)], ["vbf"])
                        for c in range(NCH):
                            K.op("pe", lambda e, c=c: e.transpose(ptr[:, c * 128:(c + 1) * 128], vbf[:, c * 128:(c + 1) * 128], ident),
                                 reads=["vbf", "cbf"], writes=["ptr"])
                        act(f2(Vt[:, :, :]), ptr[:, 0:512], AF.Copy, ["ptr"], ["Vt"])
                        nxt[0]()
                        act(S(9), pb0, AF.Sqrt, ["pb0"], [SN(9)])
                        K.op("dve", lambda e: e.tensor_scalar(out=S(9), in0=S(9), scalar1=1e-12, scalar2=None, op0=ALU.max),
                             reads=[SN(9)], writes=[SN(9)])
                        K.op("dve", lambda e: e.reciprocal(out=S(9), in_=S(9)), reads=[SN(9)], writes=[SN(9)])
                        tt("dve", S(8), S(8), S(9), ALU.mult, [SN(8), SN(9)], [SN(8)])
                        if not so:
                            tt("dve", S(16), pb1, S(V_), ALU.mult, ["pb1", SN(V_)], [SN(16)])
                        tt("dve", S(7), S(8), S(7), ALU.mult, [SN(8), SN(7)], [SN(7)])
                        for c in range(NCH):
                            cs = slice(c * 128, (c + 1) * 128)
                            K.op("dve", lambda e, cs=cs: e.tensor_tensor_scan(out=S(11)[:, cs], data0=ones128, data1=S(6)[:, cs], initial=0.0,
                                                                              op0=ALU.mult, op1=ALU.add), reads=["cf", SN(6)], writes=[SN(11)])
                        if d == 1:
                            for c in range(NCH):
                                cs = slice(c * 128, (c + 1) * 128)
                                K.op("dve", lambda e, cs=cs, c=c: e.tensor_scalar(out=S(12)[:, cs], in0=S(11)[:, cs],
                                                                                  scalar1=S(11)[:, c * 128 + 127:c * 128 + 128], scalar2=-1.0,
                                                                                  op0=ALU.subtract, op1=ALU.mult), reads=[SN(11)], writes=[SN(12)])
                            tt("dve", S(11), S(12), S(6), ALU.add, [SN(12), SN(6)], [SN(11)])
                        tt("dve", S(12), S(11), S(6), ALU.subtract, [SN(11), SN(6)], [SN(12)])
                        act(S(13), S(11), AF.Exp, [SN(11)], [SN(13)], scale=-C0)
                        act(S(14), S(11), AF.Exp, [SN(11)], [SN(14)], scale=C0)
                        act(S(15), S(12), AF.Exp, [SN(12)], [SN(15)], scale=-C0)
                        nxt[1]()
                        nxt[2]()
                        tt("dve", KR[:, :, 0, :], c3(S(8)), c3(S(15)), ALU.mult, [SN(8), SN(15)], ["KR"])
                        if not so:
                            tt("dve", KR[:, :, 1, :], c3(S(R_)), c3(S(13)), ALU.mult, [SN(R_), SN(13)], ["KR"])
                            tt("dve", R2[0:64, :, 0, :], c3(S(R_)[0:64, :]), c3(S(13)[0:64, :]), ALU.mult, [SN(R_), SN(13)], ["R2"])
                            tt("dve", R2[64:128, :, 1, :], c3(S(R_)[64:128, :]), c3(S(13)[64:128, :]), ALU.mult, [SN(R_), SN(13)], ["R2"])
                        tt("dve", BTt[:, :], S(7), S(14), ALU.mult, [SN(7), SN(14)], ["BTt"])
                        tt("dve", KTt[:, :], S(10), S(14), ALU.mult, [SN(10), SN(14)], ["KTt"])
                        nxt[3]()
                        nxt[4]()
                        for c in range(NCH):
                            K.op("pe", lambda e, c=c: e.transpose(ptr[:, 512 + c * 128:512 + (c + 1) * 128], BTt[:, c * 128:(c + 1) * 128], ident),
                                 reads=["BTt", "cbf"], writes=["ptr"])
                        K.op("dve", lambda e: e.tensor_copy(out=BKt[:, :, 0, :], in_=c3(ptr[:, 512:1024])), reads=["ptr"], writes=["BKt"])
                        for c in range(NCH):
                            K.op("pe", lambda e, c=c: e.transpose(ptr[:, c * 128:(c + 1) * 128], KTt[:, c * 128:(c + 1) * 128], ident),
                                 reads=["KTt", "cbf"], writes=["ptr"])
                        act(BKt[:, :, 1, :], c3(ptr[:, 0:512]), AF.Copy, ["ptr"], ["BKt"])
                        nxt[5]()
                        nxt[6]()
                        nxt[7]()
                        ck(5)
                        for h in range(2):
                            hp = slice(64 * h, 64 * h + 64)
                            nw_ = 128 if so else 256
                            bks = [0, 1, 3, 4]
                            for c in range(NCH):
                                cs = slice(c * 128, (c + 1) * 128)
                                bk, bn = pbank[bks[c]], "pb%d" % bks[c]
                                KRh = KR[hp, c, 0, :] if so else f2(KR[hp, c, :, :])
                                mm(bk[:, 0:nw_], BTt[hp, cs], KRh, True, True, ["BTt", "KR"], [bn])
                                mm(bk[:, 256:256 + nw_], KTt[hp, cs], KRh, True, True, ["KTt", "KR"], [bn])
                            for c in range(NCH):
                                cs = slice(c * 128, (c + 1) * 128)
                                mm(pb2[:, cs], KR[hp, c, 0, :], BTt[hp, cs], True, True, ["BTt", "KR"], ["pb2"])
                            for c in range(NCH):
                                hc = h * NCH + c
                                cs = slice(c * 128, (c + 1) * 128)
                                bk, bn = pbank[bks[c]], "pb%d" % bks[c]
                                tt("dve", MAt[:, hc, 0:nw_], bk[:, 0:nw_], mMA[:, 0:nw_], ALU.mult, [bn, "cbf"], ["MAt%d" % hc])
                                tt("dve", AKt[:, hc, 0:nw_], bk[:, 256:256 + nw_], mAK[:, 0:nw_], ALU.mult, [bn, "cbf"], ["AKt%d" % hc])
                                tt("dve", NM[:, c, 0, 0:128], pb2[:, cs], mN, ALU.mult, ["pb2", "cbf"], ["NM%d_0" % c])
                                tt("dve", XTw[:, c, 0, :], MAt[:, hc, 0:128], ident, ALU.add, ["MAt%d" % hc, "cbf"], ["XTw%d_0" % c])
                            cur = []
                            for c in range(NCH):
                                hc = h * NCH + c
                                cur.append([NM[:, c, 0, 0:128], "NM%d_0" % c, MAt[:, hc, 0:128], "MAt%d" % hc, XTw[:, c, 0, :], "XTw%d_0" % c])
                            for j in range(1, 7):
                                s = j % 2
                                for c in range(NCH):
                                    bk, bn = pbank[bks[c]], "pb%d" % bks[c]
                                    Ncur, Nname, Mcur, Mname, Xcur, Xname = cur[c]
                                    mm(bk[:, 0:128], Mcur, Ncur, True, True, [Mname, Nname], [bn])
                                    if j < 6:
                                        mm(bk[:, 128:256], Ncur, Mcur, True, True, [Mname, Nname], [bn])
                                for c in range(NCH):
                                    bk, bn = pbank[bks[c]], "pb%d" % bks[c]
                                    wd_ = 256 if j < 6 else 128
                                    act(NM[:, c, s, 0:wd_], bk[:, 0:wd_], AF.Copy, [bn], ["NM%d_%d" % (c, s)])
                                    cur[c][0], cur[c][1] = NM[:, c, s, 0:128], "NM%d_%d" % (c, s)
                                    if j < 6:
                                        cur[c][2], cur[c][3] = NM[:, c, s, 128:256], "NM%d_%d" % (c, s)
                                for c in range(NCH):
                                    bk, bn = pbank[bks[c]], "pb%d" % bks[c]
                                    mm(bk[:, 256:384], cur[c][0], cur[c][4], True, True, [cur[c][1], cur[c][5]], [bn])
                                for c in range(NCH):
                                    hc = h * NCH + c
                                    bk, bn = pbank[bks[c]], "pb%d" % bks[c]
                                    if j < 6:
                                        Xnew, Xnn = XTw[:, c, s, :], "XTw%d_%d" % (c, s)
                                    else:
                                        Xnew, Xnn = XTt[:, hc, :], "XTt%d" % hc
                                    tt("dve", Xnew, bk[:, 256:384], cur[c][4], ALU.add, [bn, cur[c][5]], [Xnn])
                                    cur[c][4], cur[c][5] = Xnew, Xnn
                        ck(6)
                        order = list(range(NCH)) if d == 0 else list(range(NCH - 1, -1, -1))
                        for c in order:
                            pidx = c * 128 + (127 if d == 0 else 0)
                            ptot = S(13)[:, pidx:pidx + 1]
                            K.op("dve", lambda e, ptot=ptot: e.tensor_scalar(out=H0P[:, :], in0=Hf[:, p, :], scalar1=ptot, scalar2=None, op0=ALU.mult),
                                 reads=["Hf", SN(13)], writes=["H0P"])
                            for h in range(2):
                                hp = slice(64 * h, 64 * h + 64)
                                hc = h * NCH + c
                                hs = slice(64 * h, 64 * h + 64)
                                mm(Xps[:, hs], AKt[:, hc, 0:128], Vt[:, c, hs], True, False, ["AKt%d" % hc, "Vt"], ["pb2"])
                                mm(Xps[:, hs], KR[hp, c, 0, :], Hb[hp, p, :], False, True, ["KR", "Hb"], ["pb2"])
                            act(Xn[:, :], Xps, AF.Copy, ["pb2"], ["Xn"], scale=-1.0)
                            for h in range(2):
                                hc = h * NCH + c
                                hs = slice(64 * h, 64 * h + 64)
                                mm(Ups[:, hs], XTt[:, hc, :], Xn[:, hs], True, True, ["XTt%d" % hc, "Xn"], ["pb2"])
                            K.op("dve", lambda e: e.tensor_copy(out=Ub[:, :], in_=Ups), reads=["pb2"], writes=["Ub"])
                            for h in range(0 if so else 2):
                                hc = h * NCH + c
                                hs = slice(64 * h, 64 * h + 64)
                                mm(yps[:, c, h, :], Hb[:, p, :], R2[:, c, h, :], True, False, ["Hb", "R2"], ["yps%d" % (c // 2)])
                                mm(yps[:, c, h, :], Ub[:, hs], MAt[:, hc, 128:256], False, False, ["Ub", "MAt%d" % hc], ["yps%d" % (c // 2)])
                                mm(yps[:, c, h, :], Vt[:, c, hs], AKt[:, hc, 128:256], False, True, ["Vt", "AKt%d" % hc], ["yps%d" % (c // 2)])
                            mm(pb1[:, 0:128], BKt[:, c, 0, :], Ub[:, :], True, False, ["BKt", "Ub"], ["pb1"])
                            mm(pb1[:, 0:128], BKt[:, c, 1, :], Vt[:, c, :], False, True, ["BKt", "Vt"], ["pb1"])
                            for h in range(2):
                                hp = slice(64 * h, 64 * h + 64)
                                K.op("dve", lambda e, hp=hp, h=h, ptot=ptot: e.scalar_tensor_tensor(
                                    out=Hf[hp, p, :], in0=pbank[1][hp, 64 * h:64 * h + 64], scalar=ptot[hp, :], in1=H0P[hp, :],
                                    op0=ALU.mult, op1=ALU.add), reads=["pb1", "H0P", SN(13)], writes=["Hf"])
                            act(Hb[:, p, :], Hf[:, p, :], AF.Copy, ["Hf"], ["Hb"])
                        ck(7)
                        if so:
                            continue
                        for h in range(2):
                            act(c3(ysb[:, h, :]), yps[:, :, h, :], AF.Copy, ["yps0", "yps1"], ["ysb"])
                        mm(pb0, sel0, ysb[:, 0, :], True, False, ["cf", "ysb"], ["pb0"])
                        mm(pb0, sel1, ysb[:, 1, :], False, True, ["cf", "ysb"], ["pb0"])
                        act(S(R_), pb0, AF.Copy, ["pb0"], [SN(R_)])
                        mm(pb1, blockavg, S(R_), True, True, ["cf", SN(R_)], ["pb1"])
                        tt("dve", S(K_), S(R_), pb1, ALU.subtract, [SN(R_), "pb1"], [SN(K_)])
                        tt("dve", S(V_), S(K_), S(K_), ALU.mult, [SN(K_)], [SN(V_)])
                        mm(pb0, blockavg, S(V_), True, True, ["cf", SN(V_)], ["pb0"])
                        K.op("dve", lambda e: e.tensor_scalar(out=S(6), in0=pb0, scalar1=GN_EPS, scalar2=None, op0=ALU.add),
                             reads=["pb0"], writes=[SN(6)])
                        act(S(6), S(6), AF.Sqrt, [SN(6)], [SN(6)])
                        K.op("dve", lambda e: e.reciprocal(out=S(6), in_=S(6)), reads=[SN(6)], writes=[SN(6)])
                        tt("dve", S(7), S(K_), S(6), ALU.mult, [SN(K_), SN(6)], [SN(7)])
                        K.op("dve", lambda e: e.tensor_scalar(out=S(7), in0=S(7), scalar1=pcol(par, "lnw", p), scalar2=pcol(par, "lnb", p),
                                                              op0=ALU.mult, op1=ALU.add), reads=[SN(7), "par"], writes=[SN(7)])
                        tt("dve", S(7), S(7), S(16), ALU.add, [SN(7), SN(16)], [SN(7)])
                        if pas == 0:
                            K.dma("pool", "yst", yscr[p, :, t0:t0 + BT], S(7), reads=[SN(7)], writes=["yscr"])
                        else:
                            K.dma("pool", "yld", S(8), yscr[p, :, t0:t0 + BT], reads=["yscr"], writes=[SN(8)])
                            tt("dve", S(7), S(7), S(8), ALU.add, [SN(7), SN(8)], [SN(7)])
                            tt("dve", yg[:, p, :], S(7), S(G_), ALU.mult, [SN(7), SN(G_)], ["yg"])
                    if pas == 0 or so:
                        continue
                    for q in range(8):
                        wt, wn = w_next()
                        inproj(wt, wn, 0, 128, pb0, "pb0", False)
                        act(S(3), pb0, AF.Copy, ["pb0"], [SN(3)])
                        hapC = inproj(wt, wn, 128, 128, pb1, "pb1", True)
                        act(S(4), pb1, AF.Copy, ["pb1"], [SN(4)])
                        act(hal[:, 0:2], hapC, AF.Copy, ["pb2"], ["hal"])
                        wt, wn = w_next()
                        hapH = inproj(wt, wn, 0, 128, pb0, "pb0", True)
                        tt("dve", uf[:, 1:BT + 1], S(4), pb0, ALU.mult, [SN(4), "pb0"], ["zf"])
                        tt("dve", uf[:, 0:1], hal[:, 0:1], hapH[:, 0:1], ALU.mult, ["hal", "pb2"], ["zf"])
                        tt("dve", uf[:, BT + 1:BT + 2], hal[:, 1:2], hapH[:, 1:2], ALU.mult, ["hal", "pb2"], ["zf"])
                        inproj(wt, wn, 128, 128, pb1, "pb1", False)
                        act(S(7), pb1, AF.Sigmoid, ["pb1"], [SN(7)])
                        tt("dve", S(7), S(7), pb1, ALU.mult, [SN(7), "pb1"], [SN(7)])
                        K.op("dve", lambda e: e.tensor_scalar(out=S(5), in0=uf[:, 0:BT], scalar1=pcol(par, "cw0", q), scalar2=None, op0=ALU.mult),
                             reads=["zf", "par"], writes=[SN(5)])
                        K.op("dve", lambda e: e.scalar_tensor_tensor(out=S(5), in0=uf[:, 1:BT + 1], scalar=pcol(par, "cw1", q), in1=S(5),
                                                                     op0=ALU.mult, op1=ALU.add), reads=["zf", "par", SN(5)], writes=[SN(5)])
                        K.op("dve", lambda e: e.scalar_tensor_tensor(out=S(5), in0=uf[:, 2:BT + 2], scalar=pcol(par, "cw2", q), in1=S(5),
                                                                     op0=ALU.mult, op1=ALU.add), reads=["zf", "par", SN(5)], writes=[SN(5)])
                        tt("dve", S(6), S(3), S(5), ALU.mult, [SN(3), SN(5)], [SN(6)])
                        tt("dve", ybg[:, q, :], S(6), S(7), ALU.mult, [SN(6), SN(7)], ["ybg"])
                    for j in range(8):
                        K.dma("sp", "wab", f2(wabt[:, :, :, :].rearrange("p a k n -> p a (k n)")), swab[j, :, :], reads=["wscr"], writes=["wab"])
                        wt, wn = w_next()
                        for g2 in range(2):
                            inproj(wt, wn, g2 * 128, 128, pbank[g2][:, :], "pb%d" % g2, False)
                            act(S(3 + g2), pbank[g2][:, :], AF.Sigmoid, ["pb%d" % g2, "par"], [SN(3 + g2)], bias=pcol(par, "gba", 2 * j + g2))
                        wt, wn = w_next()
                        for g2 in range(2):
                            inproj(wt, wn, g2 * 128, 128, pbank[g2][:, :], "pb%d" % g2, False)
                            act(S(5 + g2), pbank[g2][:, :], AF.Sigmoid, ["pb%d" % g2, "par"], [SN(5 + g2)], bias=pcol(par, "gbb", 2 * j + g2))
                        for g2 in range(2):
                            for kc in range(8):
                                mm(pb0, wabt[:, 0, kc, g2 * 128:(g2 + 1) * 128], yg[:, kc, :], kc == 0, kc == 7, ["wab", "yg"], ["pb0"])
                            tt("dve", S(7), pb0, S(3 + g2), ALU.mult, ["pb0", SN(3 + g2)], [SN(7)])
                            for kc in range(8):
                                mm(pb1, wabt[:, 1, kc, g2 * 128:(g2 + 1) * 128], ybg[:, kc, :], kc == 0, kc == 7, ["wab", "ybg"], ["pb1"])
                            tt("dve", S(8), pb1, S(5 + g2), ALU.mult, ["pb1", SN(5 + g2)], [SN(8)])
                            tt("dve", merged[:, 2 * j + g2, :], S(7), S(8), ALU.add, [SN(7), SN(8)], ["merged"])
                    def res(t_):
                        return pool[:, 4 * t_:4 * t_ + 4, :].rearrange("p s t -> p (s t)")

                    def resn(t_):
                        return [SN(4 * t_ + k) for k in range(4)]

                    for n in range(8):
                        wt, wn = w_next()
                        for t_ in range(4):
                            pbx = pbank[t_ % 2][:, 0:256]
                            for kc in range(NKC):
                                mm(pbx, merged[:, kc, t_ * 128:(t_ + 1) * 128], wt[:, kc, :], kc == 0, kc == NKC - 1, ["merged", wn], ["pb%d" % (t_ % 2)])
                            act(res(t_)[:, n * 256:(n + 1) * 256], pbx, AF.Copy, ["pb%d" % (t_ % 2)], resn(t_))
                    for t_ in range(4):
                        K.dma("sp", "xl", xt[:, :], x_d[t0 + t_ * 128:t0 + (t_ + 1) * 128, :], writes=["xt"])
                        tt("dve", res(t_), res(t_), xt[:, :], ALU.add, resn(t_) + ["xt"], resn(t_))
                        rstd_rows(res(t_), resn(t_), 128, 4, 5)
                        K.op("dve", lambda e, t_=t_: e.scalar_tensor_tensor(out=res(t_), in0=res(t_), scalar=st[:, 5:6], in1=fnw[:, :],
                                                                            op0=ALU.mult, op1=ALU.mult), reads=resn(t_) + ["st", "fnw"], writes=resn(t_))
                        K.dma("pool", "out%d" % t_, y_d[t0 + t_ * 128:t0 + (t_ + 1) * 128, :], res(t_), reads=resn(t_), writes=["ydram"])
        except _Stop:
            pass
        for eng in ["pe", "act", "dve", "pool", "sp"]:
            K.final_wait(eng)
    return nc


def _host_layout(inp, mirror=False):
    f = np.float32
    inp = dict(inp)
    W = np.asarray(inp["w_in"][0], f)
    mu_ = np.asarray(inp["mu_shift"][0], f)
    if mirror:
        perm = np.arange(NCOLS)
        perm[3072:3168], perm[3168:3264] = np.arange(3168, 3264), np.arange(3072, 3168)
        perm[3264:3360], perm[3360:3456] = np.arange(3360, 3456), np.arange(3264, 3360)
        W = W[:, perm]
        mu_ = mu_[perm[:3456]]
        for k_ in ["w0", "w_up", "a0", "a_up"]:
            inp[k_] = np.asarray(inp[k_])[:, ::-1]
        inp["conv_w"] = np.asarray(inp["conv_w"])[:, ::-1]
    inp["mu_shift"] = mu_[None]

    def blob(cols):
        Wc = W[:, cols]
        n = Wc.shape[1]
        return np.ascontiguousarray(Wc.reshape(16, 128, n).transpose(1, 0, 2).reshape(128, 16 * n))

    ar = np.arange
    out = {}
    Wl = np.zeros((2048, 512), f)
    for g in range(4):
        Wl[:, g * 128:g * 128 + 96] = W[:, 3072 + g * 96:3072 + (g + 1) * 96]
    out["w_l"] = np.ascontiguousarray(Wl.reshape(16, 128, 512).transpose(1, 0, 2).reshape(128, 16 * 512))
    wA, wB, wM = [], [], []
    for p in range(8):
        wA.append(blob(np.concatenate([ar(p * 128, p * 128 + 128), 1024 + ar(p * 128, p * 128 + 128)])))
        wA.append(blob(np.concatenate([2048 + ar(p * 128, p * 128 + 128), 3456 + ar(p * 128, p * 128 + 128)])))
        wB.append(blob(np.concatenate([4480 + ar(p * 128, p * 128 + 128), 5504 + ar(p * 128, p * 128 + 128)])))
        wB.append(blob(np.concatenate([6528 + ar(p * 128, p * 128 + 128), 7552 + ar(p * 128, p * 128 + 128)])))
        wM.append(blob(8576 + ar(p * 256, p * 256 + 256)))
        wM.append(blob(10624 + ar(p * 256, p * 256 + 256)))
    out["w_A"] = np.stack(wA)
    out["w_B"] = np.stack(wB)
    out["w_M"] = np.stack(wM)
    wa = np.asarray(inp["w_a_out"][0], f)
    wb = np.asarray(inp["w_b_out"][0], f)
    wab = []
    for j in range(8):
        a = wa[:, j * 256:(j + 1) * 256].reshape(8, 128, 256).transpose(1, 0, 2)
        b_ = wb[:, j * 256:(j + 1) * 256].reshape(8, 128, 256).transpose(1, 0, 2)
        wab.append(np.stack([a, b_], axis=1).reshape(128, 2 * 8 * 256))
    out["w_ab"] = np.ascontiguousarray(np.stack(wab))
    wo = np.asarray(inp["w_o"][0], f)
    out["w_o"] = np.ascontiguousarray(np.stack([wo[:, n * 256:(n + 1) * 256].reshape(16, 128, 256).transpose(1, 0, 2).reshape(128, 16 * 256)
                                                for n in range(8)]))
    wup_ = np.zeros((128, 4096), f)
    wup_[0:96, :] = np.concatenate([inp["w_up"][0, 0], inp["w_up"][0, 1], inp["a_up"][0, 0], inp["a_up"][0, 1]], axis=1)
    out["w_up"] = wup_
    par = np.zeros((128, NPAR), f)

    def put(name, vec, ncol):
        par[:, PC[name]:PC[name] + ncol] = np.asarray(vec, f).reshape(ncol, 128).T

    put("w0f", inp["w0"][0, 0], 8); put("w0b", inp["w0"][0, 1], 8)
    put("a0f", inp["a0"][0, 0], 8); put("a0b", inp["a0"][0, 1], 8)
    put("kk", inp["k_k"][0], 8); put("ka", inp["k_a"][0], 8); put("rk", np.asarray(inp["r_k"][0]).reshape(1024), 8)
    put("lnw", inp["ln_w"][0], 8); put("lnb", inp["ln_b"][0], 8)
    mu = np.asarray(inp["mu_shift"][0], f)
    put("mur", mu[0:1024], 8); put("muk", mu[1024:2048], 8); put("muv", mu[2048:3072], 8)
    par[0:96, PC["mul"]:PC["mul"] + 4] = mu[3072:3456].reshape(4, 96).T
    put("gba", inp["gate_bias"][0, 0], 16); put("gbb", inp["gate_bias"][0, 1], 16)
    for j in range(3):
        put("cw%d" % j, inp["conv_w"][0, j], 8)
    put("nw", inp["norm_w"][0], 16)
    out["params"] = par
    out["fnw"] = np.ascontiguousarray(np.broadcast_to(np.asarray(inp["final_norm_w"], f)[None, :], (128, D)))
    bf = ml_dtypes.bfloat16
    i = np.arange(128)
    su = (i[:, None] < i[None, :]).astype(f)
    sl = su.T.copy()
    eye = np.eye(128, dtype=f)
    iu = su + eye
    il = sl + eye
    cb = np.zeros((128, 128 + 2 * 640), f)
    cb[:, 0:128] = eye
    cb[:, 128:128 + 640] = np.concatenate([-su, iu, su, iu, -sl], axis=1)
    cb[:, 768:768 + 640] = np.concatenate([-sl, il, sl, il, -su], axis=1)
    out["cbf"] = cb.astype(bf)
    cfa = np.zeros((128, 640), f)
    blk = np.zeros((128, 128), f)
    blk[0:64, 0:64] = 1.0
    blk[64:128, 64:128] = 1.0
    cfa[:, 0:128] = blk
    cfa[:, 128:256] = blk / 64.0
    cfa[0:64, 256:320] = np.eye(64, dtype=f)
    cfa[0:64, 384 + 64:384 + 128] = np.eye(64, dtype=f)
    cfa[:, 512:640] = 1.0
    out["cf32"] = cfa
    return out


_NC_CACHE = {}


def _flags(seq_len, NT):
    NB = NT // BT
    keepL = np.zeros(NB, np.float32)
    for b in range(1, NB):
        keepL[b] = 1.0 if (b * BT) % seq_len != 0 else 0.0
    keepR = np.zeros(NB, np.float32)
    keepR[:-1] = keepL[1:]
    fl2 = np.stack([keepL, keepR])
    flp = np.zeros((128, 2 * NB), np.float32)
    flp[:, 0::2] = keepL[None, :]
    flp[:, 1::2] = keepR[None, :]
    return flp, fl2


def run_streams(streams, seq_lens, inp, NT, NBO=None, mirrors=None):
    if (NT, NBO) not in _NC_CACHE:
        _NC_CACHE[(NT, NBO)] = build_nc(NT, NBO)
    nc = _NC_CACHE[(NT, NBO)]
    mirrors = mirrors or [False] * 8
    lays = {False: _host_layout(inp, False)}
    if any(mirrors):
        lays[True] = _host_layout(inp, True)
    in_maps = []
    for c in range(8):
        flp, fl2 = _flags(seq_lens[c], NT)
        m = dict(lays[bool(mirrors[c])])
        m["x"] = np.ascontiguousarray(streams[c], dtype=np.float32)
        m["flagsP"] = flp
        m["flags2"] = fl2
        in_maps.append(m)
    res = run_bass_kernel_spmd(nc, in_maps, core_ids=list(range(8)))
    return [res.results[c]["y"] for c in range(8)]


def kernel(**inp):
    xp = np.asarray(inp["x_prompt"], np.float32)
    xsm = np.asarray(inp["x_sample"], np.float32)
    NT, NBO = 16384, 16
    H = NT // 2
    zeros = np.zeros((H, D), np.float32)
    streams = [xp[0], xp[0][::-1], xp[1], xp[1][::-1]]
    for c in range(4):
        streams.append(np.concatenate([xsm[4 * c:4 * c + 4].reshape(H, D), zeros], axis=0))
    seq_lens = [16384] * 4 + [2048] * 4
    mirrors = [False, True, False, True, False, False, False, False]
    ys = run_streams(streams, seq_lens, inp, NT, NBO, mirrors)
    y_prompt = np.stack([np.concatenate([ys[0], ys[1][::-1]], axis=0), np.concatenate([ys[2], ys[3][::-1]], axis=0)]).astype(np.float32)
    y_sample = np.concatenate([ys[4 + c].reshape(4, 2048, D) for c in range(4)], axis=0).astype(np.float32)
    return (y_prompt, y_sample)
```
